# Optimizing a Trainium2 kernel written in Bass

```python
import math
import jax, jax.numpy as jnp
from jax import lax
import numpy as np


D_MODEL = 1024
BATCH = 8
SEQ = 2048
DEPTH = 4

N_MIXERS = 2
N_META = 16
D_FF = 2816
CONV_WIDTH = 3
HEAD_DIM = 64
V_DIM = 2 * HEAD_DIM
N_HEADS = D_MODEL // (2 * HEAD_DIM)
ROT_DIM = HEAD_DIM // 4
ROPE_THETA = 500000.0
Q_BLOCK = 128
FRONT_PAD = Q_BLOCK - N_META
N_CONV_LAYERS = (DEPTH + N_MIXERS - 1) // N_MIXERS
N_ATTN_LAYERS = DEPTH // N_MIXERS
NORM_EPS = 1e-6
SUBLN_EPS = 1e-5
NEG_INF = -1e30

kernel_name = 'hybrid_conv_diffattn_macaron_trunk'


def rms_norm(t, g, eps=NORM_EPS):
    t32 = t.astype(jnp.float32)
    t32 = t32 * lax.rsqrt(jnp.mean(t32 * t32, axis=-1, keepdims=True) + eps)
    return t32.astype(t.dtype) * g.astype(t.dtype)


def swiglu(h, w_gu, w_down):
    g, u = jnp.split(h @ w_gu, 2, axis=-1)
    return (jax.nn.silu(g) * u) @ w_down


def short_conv_mixer(h, w_in, conv_w, w_out):
    b, c, xt = jnp.split(h @ w_in, 3, axis=-1)
    u = c * xt
    u = lax.conv_general_dilated(
        u, conv_w[:, None, :].astype(u.dtype),
        window_strides=(1,), padding=((CONV_WIDTH - 1, 0),),
        dimension_numbers=('NWC', 'WIO', 'NWC'),
        feature_group_count=u.shape[-1])
    return (b * u) @ w_out


def partial_rope(t, cos, sin):
    half = ROT_DIM // 2
    t1 = t[..., :half]
    t2 = t[..., half:ROT_DIM]
    c = cos[None, :, None, None, :]
    s = sin[None, :, None, None, :]
    return jnp.concatenate([t1 * c - t2 * s, t1 * s + t2 * c, t[..., ROT_DIM:]], axis=-1)


def diff_attention_mixer(h, w_qkv, lam_p, subln_g, w_o, lambda_init):
    bsz, seq_len, d = h.shape
    dt = h.dtype
    q, k, v = jnp.split(h @ w_qkv, 3, axis=-1)
    q = q.reshape(bsz, seq_len, N_HEADS, 2, HEAD_DIM)
    k = k.reshape(bsz, seq_len, N_HEADS, 2, HEAD_DIM)
    v = v.reshape(bsz, seq_len, N_HEADS, V_DIM)
    pos = jnp.arange(seq_len, dtype=jnp.float32)
    inv_freq = ROPE_THETA ** (-jnp.arange(0, ROT_DIM, 2, dtype=jnp.float32) / ROT_DIM)
    ang = pos[:, None] * inv_freq[None, :]
    cos = jnp.cos(ang).astype(dt)
    sin = jnp.sin(ang).astype(dt)
    q = partial_rope(q, cos, sin)
    k = partial_rope(k, cos, sin)
    q = jnp.pad(q, ((0, 0), (FRONT_PAD, 0), (0, 0), (0, 0), (0, 0))).transpose(0, 2, 3, 1, 4)
    k = jnp.pad(k, ((0, 0), (FRONT_PAD, 0), (0, 0), (0, 0), (0, 0))).transpose(0, 2, 3, 1, 4)
    v = jnp.pad(v, ((0, 0), (FRONT_PAD, 0), (0, 0), (0, 0))).transpose(0, 2, 1, 3)
    lp = seq_len + FRONT_PAD
    lam32 = lam_p.astype(jnp.float32)
    lam = (jnp.exp(jnp.sum(lam32[0] * lam32[1])) - jnp.exp(jnp.sum(lam32[2] * lam32[3]))
           + lambda_init)
    scale = HEAD_DIM ** -0.5
    outs = []
    for j in range(lp // Q_BLOCK):
        q0, q1 = j * Q_BLOCK, (j + 1) * Q_BLOCK
        qb = q[:, :, :, q0:q1]
        kb = k[:, :, :, :q1]
        vb = v[:, :, :q1]
        s = jnp.einsum('bhcqd,bhckd->bhcqk', qb, kb).astype(jnp.float32) * scale
        qi = jnp.arange(q0, q1)[:, None]
        ki = jnp.arange(q1)[None, :]
        mask = (ki <= qi) & (ki >= FRONT_PAD)
        p = jax.nn.softmax(jnp.where(mask, s, NEG_INF), axis=-1)
        a = p[:, :, 0] - lam * p[:, :, 1]
        outs.append(jnp.einsum('bhqk,bhkv->bhqv', a.astype(dt), vb))
    o = jnp.concatenate(outs, axis=2)[:, :, FRONT_PAD:]
    o = rms_norm(o, subln_g, SUBLN_EPS) * (1.0 - lambda_init)
    o = o.transpose(0, 2, 1, 3).reshape(bsz, seq_len, d)
    return o @ w_o


def setup_inputs(seed: int = 0) -> dict:
    key = jax.random.key(seed)
    ks = jax.random.split(key, 20)

    def nrm(k, shape, scale):
        return jax.random.normal(k, shape, jnp.float32) * scale

    def gains(k, shape):
        return 1.0 + nrm(k, shape, 0.02)

    D = D_MODEL
    return {
        'x': nrm(ks[0], (BATCH, SEQ, D), 1.0),
        'meta_tokens': nrm(ks[1], (N_META, D), 1.0),
        'ln_ffn1': gains(ks[2], (DEPTH, 2, D)),
        'ffn1_w_gu': nrm(ks[3], (DEPTH, D, 2 * D_FF), D ** -0.5),
        'ffn1_w_down': nrm(ks[4], (DEPTH, D_FF, D), D_FF ** -0.5),
        'ln_mix': gains(ks[5], (DEPTH, 2, D)),
        'conv_w_in': nrm(ks[6], (N_CONV_LAYERS, D, 3 * D), D ** -0.5),
        'conv_w': nrm(ks[7], (N_CONV_LAYERS, CONV_WIDTH, D), CONV_WIDTH ** -0.5),
        'conv_w_out': nrm(ks[8], (N_CONV_LAYERS, D, D), D ** -0.5),
        'attn_w_qkv': nrm(ks[9], (N_ATTN_LAYERS, D, 3 * D), D ** -0.5),
        'attn_lambda': nrm(ks[10], (N_ATTN_LAYERS, 4, HEAD_DIM), 0.1),
        'attn_subln_g': gains(ks[11], (N_ATTN_LAYERS, V_DIM)),
        'attn_w_o': nrm(ks[12], (N_ATTN_LAYERS, D, D), D ** -0.5),
        'ln_ffn2': gains(ks[13], (DEPTH, 2, D)),
        'ffn2_w_gu': nrm(ks[14], (DEPTH, D, 2 * D_FF), D ** -0.5),
        'ffn2_w_down': nrm(ks[15], (DEPTH, D_FF, D), D_FF ** -0.5),
    }


def reference(x, meta_tokens, ln_ffn1, ffn1_w_gu, ffn1_w_down, ln_mix,
              conv_w_in, conv_w, conv_w_out, attn_w_qkv, attn_lambda,
              attn_subln_g, attn_w_o, ln_ffn2, ffn2_w_gu, ffn2_w_down):
    bsz = x.shape[0]
    meta = jnp.broadcast_to(meta_tokens.astype(x.dtype)[None], (bsz, N_META, x.shape[-1]))
    h = jnp.concatenate([meta, x], axis=1)
    for i in range(DEPTH):
        f = swiglu(rms_norm(h, ln_ffn1[i, 0]), ffn1_w_gu[i], ffn1_w_down[i])
        h = h + 0.5 * rms_norm(f, ln_ffn1[i, 1])
        hn = rms_norm(h, ln_mix[i, 0])
        j = i // N_MIXERS
        if i % N_MIXERS == 0:
            m = short_conv_mixer(hn, conv_w_in[j], conv_w[j], conv_w_out[j])
        else:
            lambda_init = 0.8 - 0.6 * math.exp(-0.3 * i)
            m = diff_attention_mixer(hn, attn_w_qkv[j], attn_lambda[j],
                                     attn_subln_g[j], attn_w_o[j], lambda_init)
        h = h + rms_norm(m, ln_mix[i, 1])
        f = swiglu(rms_norm(h, ln_ffn2[i, 0]), ffn2_w_gu[i], ffn2_w_down[i])
        h = h + 0.5 * rms_norm(f, ln_ffn2[i, 1])
    return h[:, N_META:]
```

```python
import math
from contextlib import ExitStack

import numpy as np
import concourse.bass as bass
import concourse.mybir as mybir
from concourse.bass_utils import run_bass_kernel_spmd

F32 = mybir.dt.float32
BF16 = mybir.dt.bfloat16
ALU = mybir.AluOpType
AF = mybir.ActivationFunctionType

D = 1024
FF = 2816
SEQ = 2048
NMETA = 16
NT = SEQ + NMETA
DEPTH = 4
KC = D // 128
JC = FF // 128
HD = 64
NH = 8
ROT = 16
THETA = 500000.0
NORM_EPS = 1e-6
SUBLN_EPS = 1e-5

TILES = [(0, 16), (16, 512), (528, 512), (1040, 512), (1552, 512)]
SUPERS = [[0, 1, 2], [3, 4]]
STW = 1040
KT = [(0, 16)] + [(16 + 128 * i, 128) for i in range(16)]

ENGS = ("pe", "act", "dve", "pool", "sp")
NSLOT = 2
SLOT_EL = 8192


class _Op:
    __slots__ = ("eng", "fn", "deps", "signal", "semval", "dsem", "dval", "idx", "ndma")

    def __init__(self, eng, fn):
        self.eng = eng
        self.fn = fn
        self.deps = []
        self.signal = False
        self.semval = 0
        self.dsem = None
        self.dval = 0
        self.idx = 0
        self.ndma = 0


class Prog:
    def __init__(self):
        self.ops = {e: [] for e in ENGS}
        self.res = {}
        self.waited = {e: {} for e in ENGS}
        self.dma_cnt = {}
        self.capture = None

    def _src(self, op):
        if op.dsem is not None:
            return ("d", op.dsem), op.dval
        return ("e", op.eng), op.idx

    def _add_dep(self, op, dep):
        if dep is None or dep is op:
            return
        if dep.dsem is None and dep.eng == "pe" and op.eng == "pe" and op.dsem is None:
            return
        src, v = self._src(dep)
        w = self.waited[op.eng]
        if w.get(src, -1) >= v:
            return
        w[src] = v
        op.deps.append(dep)
        if dep.dsem is None:
            dep.signal = True

    def op(self, *a, **k):
        if self.capture is not None:
            self.capture.append((a, k))
            return None
        return self._op(*a, **k)

    def _op(self, eng, fn, r=(), w=(), dsem=None, ndma=1, war=()):
        o = _Op(eng, fn)
        lst = self.ops[eng]
        o.idx = len(lst)
        if dsem is not None:
            o.dsem = dsem
            o.ndma = ndma
            self.dma_cnt[dsem] = self.dma_cnt.get(dsem, 0) + ndma
            o.dval = 16 * self.dma_cnt[dsem]
        for k in r:
            e = self.res.get(k)
            if e is not None:
                self._add_dep(o, e[0])
        for k in tuple(w) + tuple(war):
            e = self.res.get(k)
            if e is not None:
                self._add_dep(o, e[0])
                for rd in e[1]:
                    self._add_dep(o, rd)
        for k in r:
            e = self.res.get(k)
            if e is None:
                self.res[k] = [None, [o]]
            else:
                e[1].append(o)
        for k in w:
            self.res[k] = [o, []]
        lst.append(o)
        return o

    def finalize(self):
        for e in ENGS:
            c = 0
            for o in self.ops[e]:
                if o.dsem is None and o.signal:
                    c += 1
                    o.semval = c

    def replay(self, eng, engine, sems, dsems):
        for o in self.ops[eng]:
            for d in o.deps:
                if d.dsem is not None:
                    engine.wait_ge(dsems[d.dsem], d.dval)
                else:
                    engine.wait_ge(sems[d.eng], d.semval)
            ins = o.fn(engine)
            if o.dsem is not None:
                if not isinstance(ins, (list, tuple)):
                    ins = [ins]
                assert len(ins) == o.ndma
                for i in ins:
                    i.then_inc(dsems[o.dsem], 16)
            elif o.signal:
                ins.then_inc(sems[o.eng], 1)


def lambda_init(i):
    return 0.8 - 0.6 * math.exp(-0.3 * i)


def build_program(depth=DEPTH, nsteps=None, dbg=False):
    nc = bass.Bass("TRN2", target_bir_lowering=False)
    P = Prog()

    def din(name, shape):
        return nc.dram_tensor(name, list(shape), F32, kind="ExternalInput").ap()

    x_d = din("x", (SEQ, D))
    meta_d = din("meta", (NMETA, D))
    vec_d = din("vecs", (32, D))
    lam_d = din("lam", (1, 512))
    gu1_d = din("ffn1_w_gu", (DEPTH, D, 2 * FF))
    dn1_d = din("ffn1_w_down", (DEPTH, FF, D))
    gu2_d = din("ffn2_w_gu", (DEPTH, D, 2 * FF))
    dn2_d = din("ffn2_w_down", (DEPTH, FF, D))
    cin_d = din("conv_w_in", (2, D, 3 * D))
    cout_d = din("conv_w_out", (2, D, D))
    qkv_d = din("attn_w_qkv", (2, D, 3 * D))
    wo_d = din("attn_w_o", (2, D, D))
    rope_d = din("rope", (128, 2, NT))
    cst_d = din("consts", (128, 256))
    tri_d = din("tri", (128, 128))
    out_d = nc.dram_tensor("out", [SEQ, D], F32, kind="ExternalOutput").ap()
    dbg_d = nc.dram_tensor("dbg", [128, KC * NT], F32, kind="ExternalOutput").ap() if dbg else None

    es = ExitStack()
    with es:
        sb = lambda n, s, d: es.enter_context(nc.sbuf_tensor(n, list(s), d))
        h = sb("h", (128, KC, NT), F32)
        ring = sb("ring", (128, NSLOT, SLOT_EL), BF16)
        RX = sb("RX", (128, 3 * KC * STW), BF16)
        RY = sb("RY", (128, JC * STW), BF16)
        MISC = sb("MISC", (128, 3840), F32)
        GT = sb("GT", (128, KC, 32), F32)
        cst = sb("cst", (128, 256), F32)
        onesb = sb("onesb", (128, 128), BF16)
        trib = sb("trib", (128, 128), BF16)
        lamt = sb("lamt", (128, 16), F32)
        banks = [es.enter_context(nc.psum_tensor(f"pb{i}", [128, 512], F32)) for i in range(8)]

        ident = cst[:, 0:128]
        rmat = cst[:, 128:256]

        RXb = RX[:]
        RXf = RX[:].bitcast(F32)
        RYb = RY[:]
        RYf = RY[:].bitcast(F32)
        MF = MISC[:]
        MB = MISC[:].bitcast(BF16)

        hn_st = RXb[:, 0:KC * STW].rearrange("p (k c) -> p k c", k=KC)
        f_st = RXb[:, KC * STW:3 * KC * STW].bitcast(F32).rearrange("p (k c) -> p k c", k=KC)
        hn_B = RXb[:, KC * STW:KC * STW + KC * 1024].rearrange("p (k c) -> p k c", k=KC)

        def hnv(c0, n):
            if c0 < STW:
                assert c0 + n <= STW
                return hn_st[:, :, c0:c0 + n]
            return hn_B[:, :, c0 - STW:c0 - STW + n]

        def hnkey(t):
            return ("hn", t) if t < 3 else ("hnm", t)
        rope_sb = RXb[:, KC * NT:KC * NT + 4 * NT].bitcast(F32).rearrange("p (a c) -> p a c", a=2)
        act = RYb.rearrange("p (j c) -> p j c", j=JC)
        y_all = RYb[:, 0:KC * NT].rearrange("p (k c) -> p k c", k=KC)
        rstd = [MF[:, 0:512], MF[:, 512:1024]]
        sgt = [MF[:, 1024:1536], MF[:, 1536:2048]]
        sqb = [MB[:, 4096 + 512 * i: 4096 + 512 * (i + 1)] for i in range(3)]
        vtmp = MF[:, 2816:3328]
        pT = [MB[:, 512 * i: 512 * (i + 1)] for i in range(4)]
        tq = [MF[:, 1024:1536], MF[:, 1536:2048]]
        tk = [MF[:, 2048:2560], MF[:, 2560:3072]]
        t0 = tq[0]
        t1 = tq[1]
        osq = MB[:, 6144:6656]
        rstd_a = MF[:, 3328:3840]
        u_buf = RYf[:, KC * NT // 2: KC * NT // 2 + NT + 2]
        qr = RYb[:, KC * NT: KC * NT + NT]
        kr = RYb[:, KC * NT + NT: KC * NT + 2 * NT]
        vt = RYb[:, KC * NT + 2 * NT: KC * NT + 2 * NT + 17 * 128].rearrange("p (t v) -> p t v", t=17)

        def slot_view(s, off, a, b):
            return ring[:, s, off:off + a * b].rearrange("p (a b) -> p a b", a=a)

        bar_cnt = [0]

        def barrier():
            i = bar_cnt[0]
            bar_cnt[0] += 1
            P.op("act", lambda e: e.activation(out=lamt[:, 8:9], in_=lamt[:, 8:9], func=AF.Copy),
                 w=[("bar", "act")])
            P.op("dve", lambda e: e.tensor_copy(out=lamt[:, 9:10], in_=lamt[:, 9:10]), w=[("bar", "dve")])
            P.op("pe", lambda e: e.matmul(banks[7][0:1, 0:2], ident[0:1, 0:1], ident[0:1, 0:2],
                                          start=True, stop=True),
                 r=[("bar", "act"), ("bar", "dve")], w=[("bar", "pe"), ("pb", 7)])
            P.op("act", lambda e: e.activation(out=lamt[:, 8:9], in_=lamt[:, 8:9], func=AF.Copy),
                 r=[("bar", "pe"), ("bar", "dve")], w=[("bar", "act")])
            P.op("dve", lambda e: e.tensor_copy(out=lamt[:, 9:10], in_=lamt[:, 9:10]),
                 r=[("bar", "pe"), ("bar", "act")], w=[("bar", "dve")])

        def mm_group(out_ps, pairs, r, w):
            def fn(e):
                n = len(pairs)
                ins = None
                for i, (l, rh) in enumerate(pairs):
                    ins = e.matmul(out_ps, l, rh, start=(i == 0), stop=(i == n - 1))
                return ins
            return P.op("pe", fn, r=r, w=w)

        slot_ctr = [0]

        def wfill(dmas):
            s = slot_ctr[0] % NSLOT
            slot_ctr[0] += 1
            views, srcs = [], []
            for (off, a, b, src) in dmas:
                v = slot_view(s, off, a, b)
                for a0 in range(0, a, 4):
                    a1 = min(a, a0 + 4)
                    views.append(v[:, a0:a1, :])
                    srcs.append(src[:, a0:a1, :])
            full_views = [slot_view(s, off, a, b) for (off, a, b, _) in dmas]

            def fn(e):
                return [e.dma_start(out=v, in_=sr) for v, sr in zip(views, srcs)]
            P.op("pool", fn, w=[("ring", s)], dsem=f"ws{s}", ndma=len(views))
            return s, full_views

        def norm_stats(src3, n, rs, rkeys, eps_total, kc=KC, bank=6):
            def sq(k):
                b = sqb[k % 3]
                P.op("act", lambda e, k=k, b=b: e.activation(out=b[:, 0:n], in_=src3[:, k, :], func=AF.Square),
                     r=rkeys, w=[("sqb", k % 3)])

            def mm(k):
                b = sqb[k % 3]
                P.op("pe", lambda e, k=k, b=b: e.matmul(banks[bank][:, 0:n], onesb[:, :], b[:, 0:n],
                                                         start=(k == 0), stop=(k == kc - 1)),
                     r=[("sqb", k % 3)], w=[("pb", bank)])
            sq(0)
            sq(1)
            for k in range(2, kc):
                sq(k)
                mm(k - 2)
            mm(kc - 2)
            mm(kc - 1)
            P.op("act", lambda e: e.activation(out=rs[:, 0:n], in_=banks[bank][:, 0:n], func=AF.Ln,
                                               bias=eps_total, scale=1.0),
                 r=[("pb", bank)], w=[("rstd", id(rs))])
            P.op("act", lambda e: e.activation(out=rs[:, 0:n], in_=rs[:, 0:n], func=AF.Exp, scale=-0.5),
                 r=[("rstd", id(rs))], w=[("rstd", id(rs))])

        def gvec(v, k):
            return GT[:, k, v:v + 1]

        P.op("sp", lambda e: e.dma_start(out=cst[:, :], in_=cst_d), w=["cst"], dsem="c0")
        P.op("dve", lambda e: e.memset(onesb[:, :], 1.0), w=["onesb"])
        P.op("dve", lambda e: e.memset(lamt[:, :], 0.0), w=["lamt"])
        P.op("pool", lambda e: e.dma_start(out=trib[:, :], in_=tri_d), w=["trib"], dsem="c3")
        vst = RXf[0:32, 0:D]
        P.op("sp", lambda e: e.dma_start(out=vst, in_=vec_d), w=["vst"], dsem="c1")
        for k in range(KC):
            P.op("pe", lambda e, k=k: e.transpose(banks[0][:, k * 32:(k + 1) * 32], vst[:, k * 128:(k + 1) * 128],
                                                  ident[0:32, 0:32]),
                 r=["vst", "cst"], w=[("pb", 0)])
        P.op("act", lambda e: e.activation(out=GT[:, :, :],
                                           in_=banks[0][:, 0:KC * 32].rearrange("p (k r) -> p k r", k=KC),
                                           func=AF.Copy),
             r=[("pb", 0)], w=["GT"])
        sD = math.sqrt(D)
        P.op("dve", lambda e: e.tensor_scalar(out=GT[:, :, 0:24], in0=GT[:, :, 0:24], scalar1=sD, scalar2=None,
                                              op0=ALU.mult), r=["GT"], w=["GT"])
        for lo in (1, 17):
            P.op("dve", lambda e, lo=lo: e.tensor_scalar(out=GT[:, :, lo:lo + 7:2], in0=GT[:, :, lo:lo + 7:2],
                                                         scalar1=0.5, scalar2=None, op0=ALU.mult),
                 r=["GT"], w=["GT"])
        lamw = MF[:, 0:512]
        P.op("sp", lambda e: e.dma_start(out=lamw, in_=lam_d.rearrange("a b -> (a b)").partition_broadcast(128)),
             w=["lamw"], dsem="c2")
        lam4 = lamw.rearrange("p (a b c) -> p a b c", a=2, b=4)
        for a in range(2):
            for q in range(2):
                P.op("dve", lambda e, a=a, q=q: e.tensor_tensor(out=MF[:, 512:576], in0=lam4[:, a, 2 * q, :],
                                                                in1=lam4[:, a, 2 * q + 1, :], op=ALU.mult),
                     r=["lamw"], w=["lamp"])
                P.op("dve", lambda e, a=a, q=q: e.reduce_sum(out=lamt[:, 2 * a + q:2 * a + q + 1], in_=MF[:, 512:576],
                                                             axis=mybir.AxisListType.X),
                     r=["lamp"], w=["lamt"])
        P.op("act", lambda e: e.activation(out=lamt[:, 0:4], in_=lamt[:, 0:4], func=AF.Exp), r=["lamt"], w=["lamt"])
        for a in range(2):
            li = lambda_init(2 * a + 1)
            P.op("dve", lambda e, a=a, li=li: e.scalar_tensor_tensor(out=lamt[:, 4 + a:5 + a], in0=lamt[:, 2 * a + 1:2 * a + 2],
                                                                     scalar=-li, in1=lamt[:, 2 * a:2 * a + 1],
                                                                     op0=ALU.add, op1=ALU.subtract),
                 r=["lamt"], w=["lamt"])
            P.op("dve", lambda e, a=a, li=li: e.tensor_scalar(out=GT[:, 0:1, 30 + a:31 + a], in0=GT[:, 0:1, 30 + a:31 + a],
                                                              scalar1=math.sqrt(128.0) * (1.0 - li), scalar2=None,
                                                              op0=ALU.mult), r=["GT"], w=["GT"])
        barrier()

        stg = [RYf[:, i * D:(i + 1) * D] for i in range(4)]
        blocks = [(None, 0, NMETA)] + [(i, NMETA + 128 * i, 128) for i in range(SEQ // 128)]
        for bi, (xi, c0, n) in enumerate(blocks):
            s = bi % 4
            src = meta_d if xi is None else x_d[xi * 128:(xi + 1) * 128, :]
            P.op("sp", lambda e, s=s, src=src, n=n: e.dma_start(out=stg[s][0:n, :], in_=src),
                 w=[("stg", s)], dsem=f"ld{s}")
            for half in range(2):
                bk = 2 + (2 * bi + half) % 4
                for kk in range(4):
                    k = 4 * half + kk
                    P.op("pe", lambda e, s=s, n=n, k=k, kk=kk, bk=bk: e.transpose(
                        banks[bk][:, kk * 128:kk * 128 + n], stg[s][0:n, k * 128:(k + 1) * 128], ident[0:n, 0:n]),
                        r=[("stg", s), "cst"], w=[("pb", bk)])
                eng = "act" if half == 0 else "dve"
                src_ps = banks[bk][:, :].rearrange("p (a b) -> p a b", a=4)[:, :, 0:n]
                dst = h[:, 4 * half:4 * half + 4, c0:c0 + n]
                if eng == "act":
                    P.op("act", lambda e, dst=dst, src_ps=src_ps: e.activation(out=dst, in_=src_ps, func=AF.Copy),
                         r=[("pb", bk)], w=[("h", bi)])
                else:
                    P.op("dve", lambda e, dst=dst, src_ps=src_ps: e.tensor_copy(out=dst, in_=src_ps),
                         r=[("pb", bk)], w=[("h", bi)])
        barrier()

        def hkey(t):
            return ("hT", t)

        ALLT = list(range(5))
        HNK = [("hn", t) for t in ALLT]
        FSK = [("fst", t) for t in ALLT]
        RYK = [("act", t) for t in ALLT] + [("y", t) for t in ALLT] + ["u", "u0", "vt", ("qk", 0), ("qk", 1)]

        def post_norm_add(src_st, tiles_loc, vrow):
            for (t, off) in tiles_loc:
                c0, n = TILES[t]
                rs = rstd[t % 2]
                sv = src_st[:, :, off:off + n]
                norm_stats(sv, n, rs, [("fst", t)], D * NORM_EPS, bank=6)
                for m in range(KC):
                    P.op("dve", lambda e, m=m, off=off, n=n, rs=rs: e.scalar_tensor_tensor(
                        out=src_st[:, m, off:off + n], in0=src_st[:, m, off:off + n], scalar=gvec(vrow, m),
                        in1=rs[:, 0:n], op0=ALU.mult, op1=ALU.mult),
                        r=[("fst", t), ("rstd", id(rs)), "GT"], w=[("fst", t)])
                P.op("dve", lambda e, c0=c0, n=n, sv=sv: e.tensor_tensor(out=h[:, :, c0:c0 + n], in0=h[:, :, c0:c0 + n],
                                                                         in1=sv, op=ALU.add),
                     r=[("fst", t), hkey(t)], w=[hkey(t)])

        def pre_norm(tiles_loc, dst3, vrow, war):
            for (t, off) in tiles_loc:
                c0, n = TILES[t]
                rs = rstd[t % 2]
                norm_stats(h[:, :, c0:c0 + n], n, rs, [hkey(t)], D * NORM_EPS, bank=6)
                for k in range(KC):
                    P.op("dve", lambda e, k=k, c0=c0, n=n, off=off, rs=rs: e.scalar_tensor_tensor(
                        out=dst3[:, k, off:off + n], in0=h[:, k, c0:c0 + n], scalar=gvec(vrow, k),
                        in1=rs[:, 0:n], op0=ALU.mult, op1=ALU.mult),
                        r=[hkey(t), ("rstd", id(rs)), "GT"], w=[("hn", t)], war=(war if k == 0 else ()))

        def locs(S):
            out, off = [], 0
            for t in S:
                out.append((t, off))
                off += TILES[t][1]
            return out

        sched = {"pre_done": None}
        micro_pre = []
        micro_post = []

        def capture(q, fn):
            P.capture = []
            fn()
            q.extend(P.capture)
            P.capture = None

        def drain(q, n=None):
            while q and (n is None or n > 0):
                a, k = q.pop(0)
                P._op(*a, **k)
                if n is not None:
                    n -= 1

        def flush_deferred():
            drain(micro_pre)
            drain(micro_post)

        def ffn_pre(item, lazy):
            _, wgu, wdn, vpre, vpost, S = item
            if lazy:
                capture(micro_pre, lambda: pre_norm(locs(S), hn_st, vpre, HNK + HNK_MIX))
            else:
                pre_norm(locs(S), hn_st, vpre, HNK + HNK_MIX)
            sched["pre_done"] = item

        def ffn_pass(item, nxt):
            _, wgu, wdn, vpre, vpost, S = item
            wdn_r = wdn.rearrange("(j p) c -> p j c", p=128)
            tiles_loc = locs(S)
            if sched["pre_done"] is not item:
                flush_deferred()
                ffn_pre(item, False)
            drain(micro_pre)
            gi = 0
            first_act = {t: True for t in S}
            for j0 in range(0, JC, 4):
                nj = min(4, JC - j0)
                gsrc = wgu[:, j0 * 128:(j0 + nj) * 128].rearrange("(k p) c -> p k c", p=128)
                usrc = wgu[:, FF + j0 * 128:FF + (j0 + nj) * 128].rearrange("(k p) c -> p k c", p=128)
                s, (gv, uv) = wfill([(0, KC, nj * 128, gsrc), (4096, KC, nj * 128, usrc)])
                for jl in range(nj):
                    j = j0 + jl
                    for (t, off) in tiles_loc:
                        c0, n = TILES[t]
                        pb = 2 * (gi % 2)
                        gi += 1
                        gp, up = banks[pb], banks[pb + 1]
                        mm_group(gp[:, 0:n], [(gv[:, k, jl * 128:(jl + 1) * 128], hn_st[:, k, off:off + n])
                                              for k in range(KC)],
                                 r=[("ring", s), ("hn", t)], w=[("pb", pb)])
                        mm_group(up[:, 0:n], [(uv[:, k, jl * 128:(jl + 1) * 128], hn_st[:, k, off:off + n])
                                              for k in range(KC)],
                                 r=[("ring", s), ("hn", t)], w=[("pb", pb + 1)])
                        sg = sgt[(pb // 2) % 2]
                        P.op("act", lambda e, sg=sg, gp=gp, n=n: e.activation(out=sg[:, 0:n], in_=gp[:, 0:n],
                                                                              func=AF.Silu),
                             r=[("pb", pb)], w=[("sg", pb)])
                        P.op("dve", lambda e, sg=sg, up=up, n=n, j=j, off=off: e.tensor_tensor(
                            out=act[:, j, off:off + n], in0=sg[:, 0:n], in1=up[:, 0:n], op=ALU.mult),
                            r=[("sg", pb), ("pb", pb + 1)], w=[("act", t)], war=(RYK if first_act[t] else ()))
                        first_act[t] = False
                        drain(micro_post, 2)
            drain(micro_post)
            if nxt is not None and nxt[0] == "ffn":
                ffn_pre(nxt, True)
            elif nxt is not None and nxt[0] == "mix":
                pre_norm_mixer_A(8 + 2 * nxt[1], True)
            fi = 0
            first_f = {t: True for t in S}
            for m0 in range(0, KC, 2):
                s, (dv,) = wfill([(0, JC, 256, wdn_r[:, :, m0 * 128:(m0 + 2) * 128])])
                for ml in range(2):
                    m = m0 + ml
                    for (t, off) in tiles_loc:
                        c0, n = TILES[t]
                        pb = 4 + fi % 2
                        fi += 1
                        mm_group(banks[pb][:, 0:n], [(dv[:, j, ml * 128:(ml + 1) * 128], act[:, j, off:off + n])
                                                     for j in range(JC)],
                                 r=[("ring", s), ("act", t)], w=[("pb", pb)])
                        P.op("act", lambda e, pb=pb, m=m, off=off, n=n: e.activation(
                            out=f_st[:, m, off:off + n], in_=banks[pb][:, 0:n], func=AF.Copy),
                            r=[("pb", pb)], w=[("fst", t)], war=(FSK + HNK_MIX + ["rope"] if first_f[t] else ()))
                        first_f[t] = False
                        drain(micro_pre, 4 if len(S) == 3 else 5)
            capture(micro_post, lambda: post_norm_add(f_st, tiles_loc, vpost))

        HNK_MIX = [("hnm", t) for t in ALLT]

        def mixer_out(wsrc, vpost, ykey, nxt):
            s, (wv,) = wfill([(0, KC, D, wsrc.rearrange("(k p) c -> p k c", p=128))])
            fi = 0
            prev = None
            for S in SUPERS:
                tiles_loc = locs(S)
                if tiles_loc[0][0] == 0:
                    tiles_loc = tiles_loc[1:] + tiles_loc[:1]
                for idx, (t, off) in enumerate(tiles_loc):
                    c0, n = TILES[t]
                    if idx == 0 and prev is not None:
                        drain(micro_post)
                        post_norm_add(f_st, [prev], vpost)
                        prev = None
                        if nxt is not None and nxt[0] == "ffn":
                            ffn_pre(nxt, True)
                    if prev is not None:
                        drain(micro_pre)
                        capture(micro_post, lambda pv_=prev: post_norm_add(f_st, [pv_], vpost))
                        prev = None
                    for m in range(KC):
                        pb = 4 + fi % 2
                        fi += 1
                        mm_group(banks[pb][:, 0:n], [(wv[:, k, m * 128:(m + 1) * 128], y_all[:, k, c0:c0 + n])
                                                     for k in range(KC)],
                                 r=[("ring", s), (ykey, t)], w=[("pb", pb)])
                        P.op("act", lambda e, pb=pb, m=m, off=off, n=n: e.activation(
                            out=f_st[:, m, off:off + n], in_=banks[pb][:, 0:n], func=AF.Copy),
                            r=[("pb", pb)], w=[("fst", t)], war=(FSK + HNK_MIX + ["rope"] if m == 0 else ()))
                        if micro_post:
                            drain(micro_post, 4)
                        else:
                            drain(micro_pre, 4)
                    drain(micro_post)
                    prev = (t, off)
            capture(micro_post, lambda pv_=prev: post_norm_add(f_st, [pv_], vpost))

        all_tiles = [(t, TILES[t][0]) for t in range(5)]

        def pre_norm_mixer_A(vpre, lazy):
            fn = lambda: pre_norm(locs(SUPERS[0]), hn_st, vpre, HNK + HNK_MIX)
            if lazy:
                capture(micro_pre, fn)
            else:
                fn()
            sched["pre_done"] = ("mixA", vpre)

        def pre_norm_mixer_B(vpre):
            for t in SUPERS[1]:
                c0, n = TILES[t]
                rs = rstd[t % 2]
                norm_stats(h[:, :, c0:c0 + n], n, rs, [hkey(t)], D * NORM_EPS, bank=6)
                for k in range(KC):
                    P.op("dve", lambda e, k=k, c0=c0, n=n, rs=rs: e.scalar_tensor_tensor(
                        out=hnv(c0, n)[:, k, :], in0=h[:, k, c0:c0 + n], scalar=gvec(vpre, k),
                        in1=rs[:, 0:n], op0=ALU.mult, op1=ALU.mult),
                        r=[hkey(t), ("rstd", id(rs)), "GT"], w=[("hnm", t)], war=(FSK + HNK_MIX + ["rope"] if k == 0 else ()))

        def mixer_pre(vpre):
            if sched["pre_done"] != ("mixA", vpre):
                flush_deferred()
                pre_norm_mixer_A(vpre, False)
            drain(micro_pre)

        def conv_mixer(j, vpre, vpost, nxt):
            win = cin_d[j]
            mixer_pre(vpre)
            pend_B = [True]
            P.op("dve", lambda e: e.memset(u_buf[:, 0:2], 0.0), w=["u0"], war=RYK)
            cw = lambda tap, m: GT[:, m, 24 + 3 * j + tap:24 + 3 * j + tap + 1]
            gi = 0
            first_y = {t: True for t in ALLT}
            for m0 in range(0, KC, 2):
                srcs = [win[:, sec * D + m0 * 128: sec * D + (m0 + 2) * 128].rearrange("(k p) c -> p k c", p=128)
                        for sec in range(3)]
                s, wv = wfill([(sec * 2048, KC, 256, srcs[sec]) for sec in range(3)])
                for ml in range(2):
                    m = m0 + ml
                    for (t, c0) in all_tiles:
                        n = TILES[t][1]
                        if t == 3 and pend_B[0]:
                            drain(micro_post)
                            pre_norm_mixer_B(vpre)
                            pend_B[0] = False
                        if t == 0:
                            pbs = [7, 7, 7]
                            pvs = [banks[7][:, 16 * sec:16 * sec + n] for sec in range(3)]
                        else:
                            pbs = [3 * (gi % 2), 3 * (gi % 2) + 1, 3 * (gi % 2) + 2]
                            pvs = [banks[q][:, 0:n] for q in pbs]
                            gi += 1
                        for sec in range(3):
                            mm_group(pvs[sec],
                                     [(wv[sec][:, k, ml * 128:(ml + 1) * 128], hnv(c0, n)[:, k, :]) for k in range(KC)],
                                     r=[("ring", s), hnkey(t)], w=[("pb", pbs[sec])])
                        if pend_B[0]:
                            drain(micro_post, 10)
                        bp, cp, xp = pvs
                        xs = sgt[gi % 2]
                        P.op("act", lambda e, xs=xs, xp=xp, n=n: e.activation(out=xs[:, 0:n], in_=xp[:, 0:n], func=AF.Copy),
                             r=[("pb", pbs[2])], w=[("sg", 2 * (gi % 2))])
                        P.op("dve", lambda e, xs=xs, cp=cp, n=n, c0=c0: e.tensor_tensor(
                            out=u_buf[:, 2 + c0:2 + c0 + n], in0=xs[:, 0:n], in1=cp[:, 0:n], op=ALU.mult),
                            r=[("sg", 2 * (gi % 2)), ("pb", pbs[1]), "u0"], w=["u"])
                        P.op("dve", lambda e, m=m, n=n, c0=c0: e.tensor_scalar(
                            out=vtmp[:, 0:n], in0=u_buf[:, c0:c0 + n], scalar1=cw(0, m), scalar2=None, op0=ALU.mult),
                            r=["u", "GT"], w=["vtmp"])
                        P.op("dve", lambda e, m=m, n=n, c0=c0: e.scalar_tensor_tensor(
                            out=vtmp[:, 0:n], in0=u_buf[:, 1 + c0:1 + c0 + n], scalar=cw(1, m), in1=vtmp[:, 0:n],
                            op0=ALU.mult, op1=ALU.add), r=["u", "vtmp"], w=["vtmp"])
                        P.op("dve", lambda e, m=m, n=n, c0=c0: e.scalar_tensor_tensor(
                            out=vtmp[:, 0:n], in0=u_buf[:, 2 + c0:2 + c0 + n], scalar=cw(2, m), in1=vtmp[:, 0:n],
                            op0=ALU.mult, op1=ALU.add), r=["u", "vtmp"], w=["vtmp"])
                        P.op("dve", lambda e, m=m, n=n, c0=c0, bp=bp: e.tensor_tensor(
                            out=y_all[:, m, c0:c0 + n], in0=vtmp[:, 0:n], in1=bp[:, 0:n], op=ALU.mult),
                            r=["vtmp", ("pb", pbs[0])], w=[("y", t)], war=(RYK if first_y[t] else ()))
                        first_y[t] = False
            mixer_out(cout_d[j], vpost, "y", nxt)

        def attn_mixer(j, li_idx, vpre, vpost, nxt):
            wqkv = qkv_d[j]
            neglam = lamt[:, 4 + j:5 + j]
            sgain = GT[:, 0, 30 + j:31 + j]
            mixer_pre(vpre)
            drain(micro_post)
            pre_norm_mixer_B(vpre)
            barrier()
            P.op("sp", lambda e: e.dma_start(out=rope_sb, in_=rope_d),
                 r=[("bar", "pe"), ("bar", "act"), ("bar", "dve")], w=["rope"], war=FSK, dsem="rp")
            gstep = [0]
            carry = [None]
            first_y = {t: True for t in ALLT}
            for hp in range(NH // 2):
                srcs = [wqkv[:, sec * D + hp * 256: sec * D + (hp + 1) * 256].rearrange("(k p) c -> p k c", p=128)
                        for sec in range(3)]
                s, wv = wfill([(sec * 2048, KC, 256, srcs[sec]) for sec in range(3)])
                for hl in range(2):
                    hd = 2 * hp + hl

                    porder = [1, 2, 3, 4, 0]

                    def proj(ti):
                        t, c0 = all_tiles[porder[ti]]
                        n = TILES[t][1]
                        par = (ti + 1) % 2
                        for sec in range(2):
                            pb = 2 * par + sec
                            mm_group(banks[pb][:, 0:n],
                                     [(wv[sec][:, k, hl * 128:(hl + 1) * 128], hnv(c0, n)[:, k, :]) for k in range(KC)],
                                     r=[("ring", s), hnkey(t)], w=[("pb", pb)])
                            tb = (tq, tk)[sec][par]
                            P.op("act", lambda e, pb=pb, n=n, tb=tb: e.activation(out=tb[:, 0:n], in_=banks[pb][:, 0:n],
                                                                                  func=AF.Copy),
                                 r=[("pb", pb)], w=[("tqk", sec, par)])

                    def rot(ti):
                        t, c0 = all_tiles[porder[ti]]
                        n = TILES[t][1]
                        par = (ti + 1) % 2
                        for sec, dst in ((0, qr), (1, kr)):
                            pb = 4 + 2 * par + sec
                            tb = (tq, tk)[sec][par]
                            P.op("pe", lambda e, pb=pb, n=n, tb=tb: e.matmul(banks[pb][:, 0:n], rmat, tb[:, 0:n],
                                                                             start=True, stop=True),
                                 r=[("tqk", sec, par), "cst"], w=[("pb", pb)])
                            P.op("dve", lambda e, n=n, tb=tb, c0=c0: e.tensor_tensor(
                                out=tb[:, 0:n], in0=tb[:, 0:n], in1=rope_sb[:, 0, c0:c0 + n], op=ALU.mult),
                                r=[("tqk", sec, par), "rope"], w=[("tqk", sec, par)])
                            P.op("dve", lambda e, pb=pb, n=n, c0=c0: e.tensor_tensor(
                                out=banks[pb][:, 0:n], in0=banks[pb][:, 0:n], in1=rope_sb[:, 1, c0:c0 + n], op=ALU.mult),
                                r=[("pb", pb), "rope"], w=[("pb", pb)])
                            P.op("dve", lambda e, pb=pb, n=n, c0=c0, dst=dst, tb=tb: e.tensor_tensor(
                                out=dst[:, c0:c0 + n], in0=tb[:, 0:n], in1=banks[pb][:, 0:n], op=ALU.add),
                                r=[("tqk", sec, par), ("pb", pb)], w=[("qk", sec)], war=(RYK if ti == 0 and hd == 0 else ()))

                    proj(0)
                    if carry[0] is not None:
                        carry[0]()
                        carry[0] = None
                    for ti in range(5):
                        if ti + 1 < 5:
                            proj(ti + 1)
                        rot(ti)
                    for g0 in range(0, 17, 4):
                        kts = list(range(g0, min(17, g0 + 4)))
                        pb = 0 if (g0 // 4) % 2 == 0 else 2
                        for ii, kt in enumerate(kts):
                            kc0, kn = KT[kt]
                            mm_group(banks[pb][0:kn, ii * 128:(ii + 1) * 128],
                                     [(hnv(kc0, kn)[:, k, :], wv[2][:, k, hl * 128:(hl + 1) * 128]) for k in range(KC)],
                                     r=[("ring", s)] + [hnkey(t_) for t_ in ALLT], w=[("pb", pb)])
                        for ii, kt in enumerate(kts):
                            kn = KT[kt][1]
                            P.op("act", lambda e, pb=pb, ii=ii, kt=kt, kn=kn: e.activation(
                                out=vt[0:kn, kt, :], in_=banks[pb][0:kn, ii * 128:(ii + 1) * 128], func=AF.Copy),
                                r=[("pb", pb)], w=["vt"], war=(RYK if hd == 0 and kt == 0 else ()))
                    steps = []
                    for qt in range(5):
                        qc0, qn = TILES[qt]
                        if qt == 0:
                            st = [(0, 0, qn, True)]
                        else:
                            st = [(0, 0, qn, False)]
                            st += [(1 + kt, 0, qn, False) for kt in range(4 * (qt - 1))]
                            st += [(1 + 4 * (qt - 1) + r_, 128 * r_, qn - 128 * r_, True) for r_ in range(4)]
                        for si, (kt, lo, wd, diag) in enumerate(st):
                            steps.append((qt, si, len(st), kt, lo, wd, diag))

                    def scores(gi_, step):
                        qt, si, nst, kt, lo, wd, diag = step
                        qc0, qn = TILES[qt]
                        kc0, kn = KT[kt]
                        sb_ = gi_ % 2
                        for c in range(2):
                            pb = 2 * sb_ + c
                            P.op("pe", lambda e, pb=pb, c=c, kc0=kc0, kn=kn, lo=lo, wd=wd, qc0=qc0: e.matmul(
                                banks[pb][0:kn, 0:wd], kr[c * 64:(c + 1) * 64, kc0:kc0 + kn],
                                qr[c * 64:(c + 1) * 64, qc0 + lo:qc0 + lo + wd], start=True, stop=True),
                                r=[("qk", 0), ("qk", 1)], w=[("pb", pb)])
                            pt = pT[pb]
                            P.op("act", lambda e, pb=pb, pt=pt, kn=kn, wd=wd: e.activation(
                                out=pt[0:kn, 0:wd], in_=banks[pb][0:kn, 0:wd], func=AF.Exp, scale=HD ** -0.5),
                                r=[("pb", pb)], w=[("pT", pb)])
                            if diag:
                                dn = min(kn, wd)
                                P.op("dve", lambda e, pt=pt, kn=kn, dn=dn: e.tensor_tensor(
                                    out=pt[0:kn, 0:dn], in0=pt[0:kn, 0:dn], in1=trib[0:kn, 0:dn], op=ALU.mult),
                                    r=[("pT", pb), "trib"], w=[("pT", pb)])

                    def pv(gi_, step):
                        qt, si, nst, kt, lo, wd, diag = step
                        kc0, kn = KT[kt]
                        sb_ = gi_ % 2
                        for c in range(2):
                            pb = 2 * sb_ + c
                            pt = pT[pb]
                            P.op("pe", lambda e, c=c, pt=pt, kt=kt, kn=kn, lo=lo, wd=wd, si=si, nst=nst: e.matmul(
                                banks[4 + c][:, lo:lo + wd], vt[0:kn, kt, :], pt[0:kn, 0:wd],
                                start=(si == 0), stop=(si == nst - 1)),
                                r=[("pT", pb), "vt"], w=[("pb", 4 + c)])
                            P.op("pe", lambda e, c=c, pt=pt, kn=kn, lo=lo, wd=wd, si=si, nst=nst: e.matmul(
                                banks[6 + c][:, lo:lo + wd], onesb[0:kn, :], pt[0:kn, 0:wd],
                                start=(si == 0), stop=(si == nst - 1)),
                                r=[("pT", pb), "onesb"], w=[("pb", 6 + c)])

                    def finalize(gi_, qt):
                        qc0, qn = TILES[qt]
                        k0, k1 = ("tqk", 0, 0), ("tqk", 0, 1)
                        o0, o1 = tk[0], tk[1]
                        ko0, ko1 = ("tqk", 1, 0), ("tqk", 1, 1)
                        P.op("act", lambda e, qn=qn: e.activation(out=o0[:, 0:qn], in_=banks[4][:, 0:qn], func=AF.Copy),
                             r=[("pb", 4)], w=[ko0])
                        P.op("dve", lambda e, qn=qn: e.tensor_copy(out=o1[:, 0:qn], in_=banks[5][:, 0:qn]),
                             r=[("pb", 5)], w=[ko1])
                        P.op("act", lambda e, qn=qn: e.activation(out=t1[:, 0:qn], in_=banks[7][:, 0:qn], func=AF.Copy),
                             r=[("pb", 7)], w=[k1])
                        P.op("dve", lambda e, qn=qn: e.tensor_copy(out=t0[:, 0:qn], in_=banks[6][:, 0:qn]),
                             r=[("pb", 6)], w=[k0])
                        P.op("dve", lambda e, qn=qn: e.tensor_tensor(out=o0[:, 0:qn], in0=o0[:, 0:qn], in1=t1[:, 0:qn],
                                                                     op=ALU.mult), r=[k1, ko0], w=[ko0])
                        P.op("dve", lambda e, qn=qn: e.tensor_tensor(out=o1[:, 0:qn], in0=o1[:, 0:qn], in1=t0[:, 0:qn],
                                                                     op=ALU.mult), r=[k0, ko1], w=[ko1])
                        P.op("dve", lambda e, qn=qn: e.scalar_tensor_tensor(out=o0[:, 0:qn], in0=o1[:, 0:qn], scalar=neglam,
                                                                            in1=o0[:, 0:qn], op0=ALU.mult, op1=ALU.add),
                             r=[ko0, ko1, "lamt"], w=[ko0])
                        P.op("act", lambda e, qn=qn: e.activation(out=osq[:, 0:qn], in_=o0[:, 0:qn], func=AF.Square),
                             r=[ko0], w=["osq"])
                        P.op("dve", lambda e, qn=qn: e.tensor_tensor(out=t0[:, 0:qn], in0=t0[:, 0:qn], in1=t1[:, 0:qn],
                                                                     op=ALU.mult), r=[k0, k1], w=[k0])
                        P.op("dve", lambda e, qn=qn: e.scalar_tensor_tensor(out=t0[:, 0:qn], in0=t0[:, 0:qn],
                                                                            scalar=128.0 * SUBLN_EPS, in1=t0[:, 0:qn],
                                                                            op0=ALU.mult, op1=ALU.mult),
                             r=[k0], w=[k0])

                    def finalize2(gi_, qt, hd=hd, sbk=None):
                        qc0, qn = TILES[qt]
                        k0 = ("tqk", 0, 0)
                        o0 = tk[0]
                        ko0 = ("tqk", 1, 0)
                        if sbk is None:
                            sbk = 2 * (gi_ % 2)
                        P.op("pe", lambda e, qn=qn, sbk=sbk: e.matmul(banks[sbk][:, 0:qn], onesb[:, :], osq[:, 0:qn],
                                                                      start=True, stop=True),
                             r=["osq", "onesb"], w=[("pb", sbk)])
                        P.op("dve", lambda e, qn=qn, sbk=sbk: e.tensor_tensor(out=t0[:, 0:qn], in0=t0[:, 0:qn],
                                                                              in1=banks[sbk][:, 0:qn], op=ALU.add),
                             r=[k0, ("pb", sbk)], w=[k0])
                        P.op("act", lambda e, qn=qn: e.activation(out=rstd_a[:, 0:qn], in_=t0[:, 0:qn],
                                                                  func=AF.Ln, scale=2.0 ** -20),
                             r=[k0], w=["rstd_a"])
                        P.op("act", lambda e, qn=qn: e.activation(out=rstd_a[:, 0:qn], in_=rstd_a[:, 0:qn],
                                                                  func=AF.Exp, scale=-0.5, bias=-10.0 * math.log(2.0)),
                             r=["rstd_a"], w=["rstd_a"])
                        P.op("dve", lambda e, qn=qn, qc0=qc0, hd=hd: e.scalar_tensor_tensor(
                            out=y_all[:, hd, qc0:qc0 + qn], in0=o0[:, 0:qn], scalar=sgain, in1=rstd_a[:, 0:qn],
                            op0=ALU.mult, op1=ALU.mult), r=[ko0, "rstd_a", "GT"], w=[("y", qt)],
                            war=(RYK if first_y[qt] else ()))
                        first_y[qt] = False

                    g0_ = gstep[0]
                    scores(g0_, steps[0])
                    pend = None
                    for i, step in enumerate(steps):
                        if i + 1 < len(steps):
                            scores(g0_ + i + 1, steps[i + 1])
                        pv(g0_ + i, step)
                        if pend is not None and i >= pend[0]:
                            finalize2(pend[1], pend[2])
                            pend = None
                        if step[1] == step[2] - 1:
                            finalize(g0_ + i, step[0])
                            pend = (i + 3, g0_ + i + 3, step[0])
                    if pend is not None:
                        carry[0] = (lambda f2=finalize2, a=pend[1], b=pend[2]: f2(a, b, sbk=6))
                    gstep[0] += len(steps)
            if carry[0] is not None:
                carry[0]()
                carry[0] = None
            barrier()
            if dbg:
                P.op("pool", lambda e: e.dma_start(out=dbg_d, in_=RYb[:, 0:KC * NT]), r=[("y", t) for t in range(5)], dsem="dbg")
                P.op("pool", lambda e: e.nop(), w=[("y", t) for t in range(5)])
            mixer_out(wo_d[j], vpost, "y", nxt)

        steps = []
        for i in range(depth):
            steps.append(("ffn1", i))
            steps.append(("mix", i))
            steps.append(("ffn2", i))
        if nsteps is not None:
            steps = steps[:nsteps]
        items = []
        for kind, i in steps:
            if kind == "ffn1":
                items += [("ffn", gu1_d[i], dn1_d[i], 2 * i, 2 * i + 1, S) for S in SUPERS]
            elif kind == "ffn2":
                items += [("ffn", gu2_d[i], dn2_d[i], 16 + 2 * i, 16 + 2 * i + 1, S) for S in SUPERS]
            else:
                items.append(("mix", i))
        for ii, item in enumerate(items):
            nxt = items[ii + 1] if ii + 1 < len(items) else None
            if item[0] == "ffn":
                ffn_pass(item, nxt)
            else:
                i = item[1]
                if i % 2 == 0:
                    conv_mixer(i // 2, 8 + 2 * i, 8 + 2 * i + 1, nxt)
                else:
                    attn_mixer(i // 2, i, 8 + 2 * i, 8 + 2 * i + 1, nxt)
        flush_deferred()

        barrier()
        ostg = [RYf[:, i * D:(i + 1) * D] for i in range(4)]
        for bi in range(SEQ // 128):
            c0 = NMETA + 128 * bi
            s = bi % 4
            tq = 1 + bi // 4
            for half in range(2):
                bk = (2 * bi + half) % 4
                for kk in range(4):
                    k = 4 * half + kk
                    P.op("pe", lambda e, k=k, kk=kk, bk=bk, c0=c0: e.transpose(
                        banks[bk][:, kk * 128:(kk + 1) * 128], h[:, k, c0:c0 + 128], ident),
                        r=[hkey(tq), "cst"], w=[("pb", bk)])
                dst = ostg[s][:, 512 * half:512 * (half + 1)]
                if half == 0:
                    P.op("act", lambda e, dst=dst, bk=bk: e.activation(out=dst, in_=banks[bk][:, :], func=AF.Copy),
                         r=[("pb", bk)], w=[("ostg", s)])
                else:
                    P.op("dve", lambda e, dst=dst, bk=bk: e.tensor_copy(out=dst, in_=banks[bk][:, :]),
                         r=[("pb", bk)], w=[("ostg", s)])
            P.op("sp", lambda e, s=s, bi=bi: e.dma_start(out=out_d[bi * 128:(bi + 1) * 128, :], in_=ostg[s]),
                 r=[("ostg", s)], dsem=f"st{s}")
        for s in range(4):
            P.op("sp", lambda e: e.nop(), w=[("ostg", s)])

        P.finalize()
        sems = {e: es.enter_context(nc.semaphore(f"s_{e}")) for e in ENGS}
        dsems = {n: es.enter_context(nc.semaphore(f"d_{n}")) for n in P.dma_cnt}
        with nc.Block() as block:
            @block.sync
            def _(e):
                P.replay("sp", e, sems, dsems)

            @block.scalar
            def _(e):
                P.replay("act", e, sems, dsems)

            @block.vector
            def _(e):
                P.replay("dve", e, sems, dsems)

            @block.gpsimd
            def _(e):
                P.replay("pool", e, sems, dsems)

            @block.tensor
            def _(e):
                P.replay("pe", e, sems, dsems)
    return nc


def _consts():
    ident = np.eye(128, dtype=np.float32)
    rmat = np.zeros((128, 128), np.float32)
    for c in range(2):
        for d in range(8):
            rmat[c * 64 + d + 8, c * 64 + d] = -1.0
            rmat[c * 64 + d, c * 64 + d + 8] = 1.0
    tri = (np.arange(128)[None, :] >= np.arange(128)[:, None]).astype(np.float32)
    pos = np.arange(NT, dtype=np.float32)
    inv_freq = (np.float32(THETA) ** (-np.arange(0, ROT, 2, dtype=np.float32) / np.float32(ROT))).astype(np.float32)
    ang = (pos[:, None] * inv_freq[None, :]).astype(np.float32)
    cos = np.cos(ang).astype(np.float32)
    sin = np.sin(ang).astype(np.float32)
    rope = np.zeros((128, 2, NT), np.float32)
    rope[:, 0, :] = 1.0
    for c in range(2):
        for d in range(16):
            rope[c * 64 + d, 0, :] = cos[:, d % 8]
            rope[c * 64 + d, 1, :] = sin[:, d % 8]
    return np.concatenate([ident, rmat], axis=1), tri, rope


_CACHE = {}


def kernel(x, meta_tokens, ln_ffn1, ffn1_w_gu, ffn1_w_down, ln_mix, conv_w_in, conv_w, conv_w_out,
           attn_w_qkv, attn_lambda, attn_subln_g, attn_w_o, ln_ffn2, ffn2_w_gu, ffn2_w_down,
           _depth=DEPTH, _nsteps=None, _dbg=False):
    f = lambda a: np.ascontiguousarray(np.asarray(a, dtype=np.float32))
    x = f(x)
    nb = x.shape[0]
    vecs = np.zeros((32, D), np.float32)
    vecs[0:8] = f(ln_ffn1).reshape(8, D)
    vecs[8:16] = f(ln_mix).reshape(8, D)
    vecs[16:24] = f(ln_ffn2).reshape(8, D)
    vecs[24:30] = f(conv_w).reshape(6, D)
    vecs[30:32, 0:128] = f(attn_subln_g)
    cst, tri, rope = _consts()
    shared = {
        "meta": f(meta_tokens), "vecs": vecs, "lam": f(attn_lambda).reshape(1, 512),
        "ffn1_w_gu": f(ffn1_w_gu), "ffn1_w_down": f(ffn1_w_down),
        "ffn2_w_gu": f(ffn2_w_gu), "ffn2_w_down": f(ffn2_w_down),
        "conv_w_in": f(conv_w_in), "conv_w_out": f(conv_w_out),
        "attn_w_qkv": f(attn_w_qkv), "attn_w_o": f(attn_w_o),
        "rope": rope, "consts": cst, "tri": tri,
    }
    key = (_depth, _nsteps, _dbg)
    if key not in _CACHE:
        _CACHE[key] = build_program(_depth, _nsteps, _dbg)
    nc = _CACHE[key]
    in_maps = [dict(shared, x=x[b]) for b in range(nb)]
    res = run_bass_kernel_spmd(nc, in_maps, core_ids=list(range(nb)))
    if _dbg:
        return np.stack([r["out"] for r in res.results], axis=0).astype(np.float32), res.results[0]["dbg"]
    return np.stack([r["out"] for r in res.results], axis=0).astype(np.float32)
```

```python
import math
from contextlib import ExitStack

import numpy as np
import concourse.bass as bass
import concourse.mybir as mybir
from concourse.bass_utils import run_bass_kernel_spmd

F32 = mybir.dt.float32
BF16 = mybir.dt.bfloat16
ALU = mybir.AluOpType
AF = mybir.ActivationFunctionType

D = 1024
FF = 2816
SEQ = 2048
NMETA = 16
NT = SEQ + NMETA
DEPTH = 4
KC = D // 128
JC = FF // 128
HD = 64
NH = 8
ROT = 16
THETA = 500000.0
NORM_EPS = 1e-6
SUBLN_EPS = 1e-5

TILES = [(0, 16), (16, 512), (528, 512), (1040, 512), (1552, 512)]
SUPERS = [[0, 1, 2], [3, 4]]
STW = 1040
KT = [(0, 16)] + [(16 + 128 * i, 128) for i in range(16)]

ENGS = ("pe", "act", "dve", "pool", "sp")
NSLOT = 2
SLOT_EL = 8192


class _Op:
    __slots__ = ("eng", "fn", "deps", "signal", "semval", "dsem", "dval", "idx", "ndma")

    def __init__(self, eng, fn):
        self.eng = eng
        self.fn = fn
        self.deps = []
        self.signal = False
        self.semval = 0
        self.dsem = None
        self.dval = 0
        self.idx = 0
        self.ndma = 0


class Prog:
    def __init__(self):
        self.ops = {e: [] for e in ENGS}
        self.res = {}
        self.waited = {e: {} for e in ENGS}
        self.dma_cnt = {}
        self.capture = None

    def _src(self, op):
        if op.dsem is not None:
            return ("d", op.dsem), op.dval
        return ("e", op.eng), op.idx

    def _add_dep(self, op, dep):
        if dep is None or dep is op:
            return
        if dep.dsem is None and dep.eng == "pe" and op.eng == "pe" and op.dsem is None:
            return
        src, v = self._src(dep)
        w = self.waited[op.eng]
        if w.get(src, -1) >= v:
            return
        w[src] = v
        op.deps.append(dep)
        if dep.dsem is None:
            dep.signal = True

    def op(self, *a, **k):
        if self.capture is not None:
            self.capture.append((a, k))
            return None
        return self._op(*a, **k)

    def _op(self, eng, fn, r=(), w=(), dsem=None, ndma=1, war=()):
        o = _Op(eng, fn)
        lst = self.ops[eng]
        o.idx = len(lst)
        if dsem is not None:
            o.dsem = dsem
            o.ndma = ndma
            self.dma_cnt[dsem] = self.dma_cnt.get(dsem, 0) + ndma
            o.dval = 16 * self.dma_cnt[dsem]
        for k in r:
            e = self.res.get(k)
            if e is not None:
                self._add_dep(o, e[0])
        for k in tuple(w) + tuple(war):
            e = self.res.get(k)
            if e is not None:
                self._add_dep(o, e[0])
                for rd in e[1]:
                    self._add_dep(o, rd)
        for k in r:
            e = self.res.get(k)
            if e is None:
                self.res[k] = [None, [o]]
            else:
                e[1].append(o)
        for k in w:
            self.res[k] = [o, []]
        lst.append(o)
        return o

    def finalize(self):
        for e in ENGS:
            c = 0
            for o in self.ops[e]:
                if o.dsem is None and o.signal:
                    c += 1
                    o.semval = c

    def replay(self, eng, engine, sems, dsems):
        for o in self.ops[eng]:
            for d in o.deps:
                if d.dsem is not None:
                    engine.wait_ge(dsems[d.dsem], d.dval)
                else:
                    engine.wait_ge(sems[d.eng], d.semval)
            ins = o.fn(engine)
            if o.dsem is not None:
                if not isinstance(ins, (list, tuple)):
                    ins = [ins]
                assert len(ins) == o.ndma
                for i in ins:
                    i.then_inc(dsems[o.dsem], 16)
            elif o.signal:
                ins.then_inc(sems[o.eng], 1)


def lambda_init(i):
    return 0.8 - 0.6 * math.exp(-0.3 * i)


def build_program(depth=DEPTH, nsteps=None, dbg=False):
    nc = bass.Bass("TRN2", target_bir_lowering=False)
    P = Prog()

    def din(name, shape):
        return nc.dram_tensor(name, list(shape), F32, kind="ExternalInput").ap()

    x_d = din("x", (SEQ, D))
    meta_d = din("meta", (NMETA, D))
    vec_d = din("vecs", (32, D))
    lam_d = din("lam", (1, 512))
    gu1_d = din("ffn1_w_gu", (DEPTH, D, 2 * FF))
    dn1_d = din("ffn1_w_down", (DEPTH, FF, D))
    gu2_d = din("ffn2_w_gu", (DEPTH, D, 2 * FF))
    dn2_d = din("ffn2_w_down", (DEPTH, FF, D))
    cin_d = din("conv_w_in", (2, D, 3 * D))
    cout_d = din("conv_w_out", (2, D, D))
    qkv_d = din("attn_w_qkv", (2, D, 3 * D))
    wo_d = din("attn_w_o", (2, D, D))
    rope_d = din("rope", (128, 2, NT))
    cst_d = din("consts", (128, 256))
    tri_d = din("tri", (128, 128))
    out_d = nc.dram_tensor("out", [SEQ, D], F32, kind="ExternalOutput").ap()
    dbg_d = nc.dram_tensor("dbg", [128, KC * NT], F32, kind="ExternalOutput").ap() if dbg else None

    es = ExitStack()
    with es:
        sb = lambda n, s, d: es.enter_context(nc.sbuf_tensor(n, list(s), d))
        h = sb("h", (128, KC, NT), F32)
        ring = sb("ring", (128, NSLOT, SLOT_EL), BF16)
        RX = sb("RX", (128, 3 * KC * STW), BF16)
        RY = sb("RY", (128, JC * STW), BF16)
        MISC = sb("MISC", (128, 3840), F32)
        GT = sb("GT", (128, KC, 32), F32)
        cst = sb("cst", (128, 256), F32)
        onesb = sb("onesb", (128, 128), BF16)
        trib = sb("trib", (128, 128), BF16)
        lamt = sb("lamt", (128, 16), F32)
        banks = [es.enter_context(nc.psum_tensor(f"pb{i}", [128, 512], F32)) for i in range(8)]

        ident = cst[:, 0:128]
        rmat = cst[:, 128:256]

        RXb = RX[:]
        RXf = RX[:].bitcast(F32)
        RYb = RY[:]
        RYf = RY[:].bitcast(F32)
        MF = MISC[:]
        MB = MISC[:].bitcast(BF16)

        hn_st = RXb[:, 0:KC * STW].rearrange("p (k c) -> p k c", k=KC)
        f_st = RXb[:, KC * STW:3 * KC * STW].bitcast(F32).rearrange("p (k c) -> p k c", k=KC)
        hn_B = RXb[:, KC * STW:KC * STW + KC * 1024].rearrange("p (k c) -> p k c", k=KC)

        def hnv(c0, n):
            if c0 < STW:
                assert c0 + n <= STW
                return hn_st[:, :, c0:c0 + n]
            return hn_B[:, :, c0 - STW:c0 - STW + n]

        def hnkey(t):
            return ("hn", t) if t < 3 else ("hnm", t)
        rope_sb = RXb[:, KC * NT:KC * NT + 4 * NT].bitcast(F32).rearrange("p (a c) -> p a c", a=2)
        act = RYb.rearrange("p (j c) -> p j c", j=JC)
        y_all = RYb[:, 0:KC * NT].rearrange("p (k c) -> p k c", k=KC)
        rstd = [MF[:, 0:512], MF[:, 512:1024]]
        sgt = [MF[:, 1024:1536], MF[:, 1536:2048]]
        sqb = [MB[:, 4096 + 512 * i: 4096 + 512 * (i + 1)] for i in range(3)]
        vtmp = MF[:, 2816:3328]
        pT = [MB[:, 512 * i: 512 * (i + 1)] for i in range(4)]
        tq = [MF[:, 1024:1536], MF[:, 1536:2048]]
        tk = [MF[:, 2048:2560], MF[:, 2560:3072]]
        t0 = tq[0]
        t1 = tq[1]
        osq = MB[:, 6144:6656]
        rstd_a = MF[:, 3328:3840]
        u_buf = RYf[:, KC * NT // 2: KC * NT // 2 + NT + 2]
        qr = RYb[:, KC * NT: KC * NT + NT]
        kr = RYb[:, KC * NT + NT: KC * NT + 2 * NT]
        vt = RYb[:, KC * NT + 2 * NT: KC * NT + 2 * NT + 17 * 128].rearrange("p (t v) -> p t v", t=17)

        def slot_view(s, off, a, b):
            return ring[:, s, off:off + a * b].rearrange("p (a b) -> p a b", a=a)

        bar_cnt = [0]

        def barrier():
            i = bar_cnt[0]
            bar_cnt[0] += 1
            P.op("act", lambda e: e.activation(out=lamt[:, 8:9], in_=lamt[:, 8:9], func=AF.Copy),
                 w=[("bar", "act")])
            P.op("dve", lambda e: e.tensor_copy(out=lamt[:, 9:10], in_=lamt[:, 9:10]), w=[("bar", "dve")])
            P.op("pe", lambda e: e.matmul(banks[7][0:1, 0:2], ident[0:1, 0:1], ident[0:1, 0:2],
                                          start=True, stop=True),
                 r=[("bar", "act"), ("bar", "dve")], w=[("bar", "pe"), ("pb", 7)])
            P.op("act", lambda e: e.activation(out=lamt[:, 8:9], in_=lamt[:, 8:9], func=AF.Copy),
                 r=[("bar", "pe"), ("bar", "dve")], w=[("bar", "act")])
            P.op("dve", lambda e: e.tensor_copy(out=lamt[:, 9:10], in_=lamt[:, 9:10]),
                 r=[("bar", "pe"), ("bar", "act")], w=[("bar", "dve")])

        def mm_group(out_ps, pairs, r, w):
            def fn(e):
                n = len(pairs)
                ins = None
                for i, (l, rh) in enumerate(pairs):
                    ins = e.matmul(out_ps, l, rh, start=(i == 0), stop=(i == n - 1))
                return ins
            return P.op("pe", fn, r=r, w=w)

        slot_ctr = [0]

        def wfill(dmas):
            s = slot_ctr[0] % NSLOT
            slot_ctr[0] += 1
            views, srcs = [], []
            for (off, a, b, src) in dmas:
                v = slot_view(s, off, a, b)
                for a0 in range(0, a, 4):
                    a1 = min(a, a0 + 4)
                    views.append(v[:, a0:a1, :])
                    srcs.append(src[:, a0:a1, :])
            full_views = [slot_view(s, off, a, b) for (off, a, b, _) in dmas]

            def fn(e):
                return [e.dma_start(out=v, in_=sr) for v, sr in zip(views, srcs)]
            P.op("pool", fn, w=[("ring", s)], dsem=f"ws{s}", ndma=len(views))
            return s, full_views

        def norm_stats(src3, n, rs, rkeys, eps_total, kc=KC, bank=6):
            def sq(k):
                b = sqb[k % 3]
                P.op("act", lambda e, k=k, b=b: e.activation(out=b[:, 0:n], in_=src3[:, k, :], func=AF.Square),
                     r=rkeys, w=[("sqb", k % 3)])

            def mm(k):
                b = sqb[k % 3]
                P.op("pe", lambda e, k=k, b=b: e.matmul(banks[bank][:, 0:n], onesb[:, :], b[:, 0:n],
                                                         start=(k == 0), stop=(k == kc - 1)),
                     r=[("sqb", k % 3)], w=[("pb", bank)])
            sq(0)
            sq(1)
            for k in range(2, kc):
                sq(k)
                mm(k - 2)
            mm(kc - 2)
            mm(kc - 1)
            P.op("act", lambda e: e.activation(out=rs[:, 0:n], in_=banks[bank][:, 0:n], func=AF.Ln,
                                               bias=eps_total, scale=1.0),
                 r=[("pb", bank)], w=[("rstd", id(rs))])
            P.op("act", lambda e: e.activation(out=rs[:, 0:n], in_=rs[:, 0:n], func=AF.Exp, scale=-0.5),
                 r=[("rstd", id(rs))], w=[("rstd", id(rs))])

        def gvec(v, k):
            return GT[:, k, v:v + 1]

        P.op("sp", lambda e: e.dma_start(out=cst[:, :], in_=cst_d), w=["cst"], dsem="c0")
        P.op("dve", lambda e: e.memset(onesb[:, :], 1.0), w=["onesb"])
        P.op("dve", lambda e: e.memset(lamt[:, :], 0.0), w=["lamt"])
        P.op("pool", lambda e: e.dma_start(out=trib[:, :], in_=tri_d), w=["trib"], dsem="c3")
        vst = RXf[0:32, 0:D]
        P.op("sp", lambda e: e.dma_start(out=vst, in_=vec_d), w=["vst"], dsem="c1")
        for k in range(KC):
            P.op("pe", lambda e, k=k: e.transpose(banks[0][:, k * 32:(k + 1) * 32], vst[:, k * 128:(k + 1) * 128],
                                                  ident[0:32, 0:32]),
                 r=["vst", "cst"], w=[("pb", 0)])
        P.op("act", lambda e: e.activation(out=GT[:, :, :],
                                           in_=banks[0][:, 0:KC * 32].rearrange("p (k r) -> p k r", k=KC),
                                           func=AF.Copy),
             r=[("pb", 0)], w=["GT"])
        sD = math.sqrt(D)
        P.op("dve", lambda e: e.tensor_scalar(out=GT[:, :, 0:24], in0=GT[:, :, 0:24], scalar1=sD, scalar2=None,
                                              op0=ALU.mult), r=["GT"], w=["GT"])
        for lo in (1, 17):
            P.op("dve", lambda e, lo=lo: e.tensor_scalar(out=GT[:, :, lo:lo + 7:2], in0=GT[:, :, lo:lo + 7:2],
                                                         scalar1=0.5, scalar2=None, op0=ALU.mult),
                 r=["GT"], w=["GT"])
        lamw = MF[:, 0:512]
        P.op("sp", lambda e: e.dma_start(out=lamw, in_=lam_d.rearrange("a b -> (a b)").partition_broadcast(128)),
             w=["lamw"], dsem="c2")
        lam4 = lamw.rearrange("p (a b c) -> p a b c", a=2, b=4)
        for a in range(2):
            for q in range(2):
                P.op("dve", lambda e, a=a, q=q: e.tensor_tensor(out=MF[:, 512:576], in0=lam4[:, a, 2 * q, :],
                                                                in1=lam4[:, a, 2 * q + 1, :], op=ALU.mult),
                     r=["lamw"], w=["lamp"])
                P.op("dve", lambda e, a=a, q=q: e.reduce_sum(out=lamt[:, 2 * a + q:2 * a + q + 1], in_=MF[:, 512:576],
                                                             axis=mybir.AxisListType.X),
                     r=["lamp"], w=["lamt"])
        P.op("act", lambda e: e.activation(out=lamt[:, 0:4], in_=lamt[:, 0:4], func=AF.Exp), r=["lamt"], w=["lamt"])
        for a in range(2):
            li = lambda_init(2 * a + 1)
            P.op("dve", lambda e, a=a, li=li: e.scalar_tensor_tensor(out=lamt[:, 4 + a:5 + a], in0=lamt[:, 2 * a + 1:2 * a + 2],
                                                                     scalar=-li, in1=lamt[:, 2 * a:2 * a + 1],
                                                                     op0=ALU.add, op1=ALU.subtract),
                 r=["lamt"], w=["lamt"])
            P.op("dve", lambda e, a=a, li=li: e.tensor_scalar(out=GT[:, 0:1, 30 + a:31 + a], in0=GT[:, 0:1, 30 + a:31 + a],
                                                              scalar1=math.sqrt(128.0) * (1.0 - li), scalar2=None,
                                                              op0=ALU.mult), r=["GT"], w=["GT"])
        barrier()

        stg = [RYf[:, i * D:(i + 1) * D] for i in range(4)]
        blocks = [(None, 0, NMETA)] + [(i, NMETA + 128 * i, 128) for i in range(SEQ // 128)]
        for bi, (xi, c0, n) in enumerate(blocks):
            s = bi % 4
            src = meta_d if xi is None else x_d[xi * 128:(xi + 1) * 128, :]
            P.op("sp", lambda e, s=s, src=src, n=n: e.dma_start(out=stg[s][0:n, :], in_=src),
                 w=[("stg", s)], dsem=f"ld{s}")
            for half in range(2):
                bk = 2 + (2 * bi + half) % 4
                for kk in range(4):
                    k = 4 * half + kk
                    P.op("pe", lambda e, s=s, n=n, k=k, kk=kk, bk=bk: e.transpose(
                        banks[bk][:, kk * 128:kk * 128 + n], stg[s][0:n, k * 128:(k + 1) * 128], ident[0:n, 0:n]),
                        r=[("stg", s), "cst"], w=[("pb", bk)])
                eng = "act" if half == 0 else "dve"
                src_ps = banks[bk][:, :].rearrange("p (a b) -> p a b", a=4)[:, :, 0:n]
                dst = h[:, 4 * half:4 * half + 4, c0:c0 + n]
                if eng == "act":
                    P.op("act", lambda e, dst=dst, src_ps=src_ps: e.activation(out=dst, in_=src_ps, func=AF.Copy),
                         r=[("pb", bk)], w=[("h", bi)])
                else:
                    P.op("dve", lambda e, dst=dst, src_ps=src_ps: e.tensor_copy(out=dst, in_=src_ps),
                         r=[("pb", bk)], w=[("h", bi)])
        barrier()

        def hkey(t):
            return ("hT", t)

        ALLT = list(range(5))
        HNK = [("hn", t) for t in ALLT]
        FSK = [("fst", t) for t in ALLT]
        RYK = [("act", t) for t in ALLT] + [("y", t) for t in ALLT] + ["u", "u0", "vt", ("qk", 0), ("qk", 1)]

        def post_norm_add(src_st, tiles_loc, vrow):
            for (t, off) in tiles_loc:
                c0, n = TILES[t]
                rs = rstd[t % 2]
                sv = src_st[:, :, off:off + n]
                norm_stats(sv, n, rs, [("fst", t)], D * NORM_EPS, bank=6)
                for m in range(KC):
                    P.op("dve", lambda e, m=m, off=off, n=n, rs=rs: e.scalar_tensor_tensor(
                        out=src_st[:, m, off:off + n], in0=src_st[:, m, off:off + n], scalar=gvec(vrow, m),
                        in1=rs[:, 0:n], op0=ALU.mult, op1=ALU.mult),
                        r=[("fst", t), ("rstd", id(rs)), "GT"], w=[("fst", t)])
                P.op("dve", lambda e, c0=c0, n=n, sv=sv: e.tensor_tensor(out=h[:, :, c0:c0 + n], in0=h[:, :, c0:c0 + n],
                                                                         in1=sv, op=ALU.add),
                     r=[("fst", t), hkey(t)], w=[hkey(t)])

        def pre_norm(tiles_loc, dst3, vrow, war):
            for (t, off) in tiles_loc:
                c0, n = TILES[t]
                rs = rstd[t % 2]
                norm_stats(h[:, :, c0:c0 + n], n, rs, [hkey(t)], D * NORM_EPS, bank=6)
                for k in range(KC):
                    P.op("dve", lambda e, k=k, c0=c0, n=n, off=off, rs=rs: e.scalar_tensor_tensor(
                        out=dst3[:, k, off:off + n], in0=h[:, k, c0:c0 + n], scalar=gvec(vrow, k),
                        in1=rs[:, 0:n], op0=ALU.mult, op1=ALU.mult),
                        r=[hkey(t), ("rstd", id(rs)), "GT"], w=[("hn", t)], war=(war if k == 0 else ()))

        def locs(S):
            out, off = [], 0
            for t in S:
                out.append((t, off))
                off += TILES[t][1]
            return out

        sched = {"pre_done": None}
        micro_pre = []
        micro_post = []

        def capture(q, fn):
            P.capture = []
            fn()
            q.extend(P.capture)
            P.capture = None

        def drain(q, n=None):
            while q and (n is None or n > 0):
                a, k = q.pop(0)
                P._op(*a, **k)
                if n is not None:
                    n -= 1

        def flush_deferred():
            drain(micro_pre)
            drain(micro_post)

        def ffn_pre(item, lazy):
            _, wgu, wdn, vpre, vpost, S = item
            if lazy:
                capture(micro_pre, lambda: pre_norm(locs(S), hn_st, vpre, HNK + HNK_MIX))
            else:
                pre_norm(locs(S), hn_st, vpre, HNK + HNK_MIX)
            sched["pre_done"] = item

        def ffn_pass(item, nxt):
            _, wgu, wdn, vpre, vpost, S = item
            wdn_r = wdn.rearrange("(j p) c -> p j c", p=128)
            tiles_loc = locs(S)
            if sched["pre_done"] is not item:
                flush_deferred()
                ffn_pre(item, False)
            drain(micro_pre)
            gi = 0
            first_act = {t: True for t in S}
            for j0 in range(0, JC, 4):
                nj = min(4, JC - j0)
                gsrc = wgu[:, j0 * 128:(j0 + nj) * 128].rearrange("(k p) c -> p k c", p=128)
                usrc = wgu[:, FF + j0 * 128:FF + (j0 + nj) * 128].rearrange("(k p) c -> p k c", p=128)
                s, (gv, uv) = wfill([(0, KC, nj * 128, gsrc), (4096, KC, nj * 128, usrc)])
                for jl in range(nj):
                    j = j0 + jl
                    for (t, off) in tiles_loc:
                        c0, n = TILES[t]
                        pb = 2 * (gi % 2)
                        gi += 1
                        gp, up = banks[pb], banks[pb + 1]
                        mm_group(gp[:, 0:n], [(gv[:, k, jl * 128:(jl + 1) * 128], hn_st[:, k, off:off + n])
                                              for k in range(KC)],
                                 r=[("ring", s), ("hn", t)], w=[("pb", pb)])
                        mm_group(up[:, 0:n], [(uv[:, k, jl * 128:(jl + 1) * 128], hn_st[:, k, off:off + n])
                                              for k in range(KC)],
                                 r=[("ring", s), ("hn", t)], w=[("pb", pb + 1)])
                        sg = sgt[(pb // 2) % 2]
                        P.op("act", lambda e, sg=sg, gp=gp, n=n: e.activation(out=sg[:, 0:n], in_=gp[:, 0:n],
                                                                              func=AF.Silu),
                             r=[("pb", pb)], w=[("sg", pb)])
                        P.op("dve", lambda e, sg=sg, up=up, n=n, j=j, off=off: e.tensor_tensor(
                            out=act[:, j, off:off + n], in0=sg[:, 0:n], in1=up[:, 0:n], op=ALU.mult),
                            r=[("sg", pb), ("pb", pb + 1)], w=[("act", t)], war=(RYK if first_act[t] else ()))
                        first_act[t] = False
                        drain(micro_post, 2)
            drain(micro_post)
            if nxt is not None and nxt[0] == "ffn":
                ffn_pre(nxt, True)
            elif nxt is not None and nxt[0] == "mix":
                pre_norm_mixer_A(8 + 2 * nxt[1], True)
            fi = 0
            first_f = {t: True for t in S}
            for m0 in range(0, KC, 2):
                s, (dv,) = wfill([(0, JC, 256, wdn_r[:, :, m0 * 128:(m0 + 2) * 128])])
                for ml in range(2):
                    m = m0 + ml
                    for (t, off) in tiles_loc:
                        c0, n = TILES[t]
                        pb = 4 + fi % 2
                        fi += 1
                        mm_group(banks[pb][:, 0:n], [(dv[:, j, ml * 128:(ml + 1) * 128], act[:, j, off:off + n])
                                                     for j in range(JC)],
                                 r=[("ring", s), ("act", t)], w=[("pb", pb)])
                        P.op("act", lambda e, pb=pb, m=m, off=off, n=n: e.activation(
                            out=f_st[:, m, off:off + n], in_=banks[pb][:, 0:n], func=AF.Copy),
                            r=[("pb", pb)], w=[("fst", t)], war=(FSK + HNK_MIX + ["rope"] if first_f[t] else ()))
                        first_f[t] = False
                        drain(micro_pre, 4 if len(S) == 3 else 5)
            capture(micro_post, lambda: post_norm_add(f_st, tiles_loc, vpost))

        HNK_MIX = [("hnm", t) for t in ALLT]

        def mixer_out(wsrc, vpost, ykey, nxt):
            s, (wv,) = wfill([(0, KC, D, wsrc.rearrange("(k p) c -> p k c", p=128))])
            fi = 0
            prev = None
            for S in SUPERS:
                tiles_loc = locs(S)
                if tiles_loc[0][0] == 0:
                    tiles_loc = tiles_loc[1:] + tiles_loc[:1]
                for idx, (t, off) in enumerate(tiles_loc):
                    c0, n = TILES[t]
                    if idx == 0 and prev is not None:
                        drain(micro_post)
                        post_norm_add(f_st, [prev], vpost)
                        prev = None
                        if nxt is not None and nxt[0] == "ffn":
                            ffn_pre(nxt, True)
                    if prev is not None:
                        drain(micro_pre)
                        capture(micro_post, lambda pv_=prev: post_norm_add(f_st, [pv_], vpost))
                        prev = None
                    for m in range(KC):
                        pb = 4 + fi % 2
                        fi += 1
                        mm_group(banks[pb][:, 0:n], [(wv[:, k, m * 128:(m + 1) * 128], y_all[:, k, c0:c0 + n])
                                                     for k in range(KC)],
                                 r=[("ring", s), (ykey, t)], w=[("pb", pb)])
                        P.op("act", lambda e, pb=pb, m=m, off=off, n=n: e.activation(
                            out=f_st[:, m, off:off + n], in_=banks[pb][:, 0:n], func=AF.Copy),
                            r=[("pb", pb)], w=[("fst", t)], war=(FSK + HNK_MIX + ["rope"] if m == 0 else ()))
                        if micro_post:
                            drain(micro_post, 4)
                        else:
                            drain(micro_pre, 4)
                    drain(micro_post)
                    prev = (t, off)
            capture(micro_post, lambda pv_=prev: post_norm_add(f_st, [pv_], vpost))

        all_tiles = [(t, TILES[t][0]) for t in range(5)]

        def pre_norm_mixer_A(vpre, lazy):
            fn = lambda: pre_norm(locs(SUPERS[0]), hn_st, vpre, HNK + HNK_MIX)
            if lazy:
                capture(micro_pre, fn)
            else:
                fn()
            sched["pre_done"] = ("mixA", vpre)

        def pre_norm_mixer_B(vpre):
            for t in SUPERS[1]:
                c0, n = TILES[t]
                rs = rstd[t % 2]
                norm_stats(h[:, :, c0:c0 + n], n, rs, [hkey(t)], D * NORM_EPS, bank=6)
                for k in range(KC):
                    P.op("dve", lambda e, k=k, c0=c0, n=n, rs=rs: e.scalar_tensor_tensor(
                        out=hnv(c0, n)[:, k, :], in0=h[:, k, c0:c0 + n], scalar=gvec(vpre, k),
                        in1=rs[:, 0:n], op0=ALU.mult, op1=ALU.mult),
                        r=[hkey(t), ("rstd", id(rs)), "GT"], w=[("hnm", t)], war=(FSK + HNK_MIX + ["rope"] if k == 0 else ()))

        def mixer_pre(vpre):
            if sched["pre_done"] != ("mixA", vpre):
                flush_deferred()
                pre_norm_mixer_A(vpre, False)
            drain(micro_pre)

        def conv_mixer(j, vpre, vpost, nxt):
            win = cin_d[j]
            mixer_pre(vpre)
            pend_B = [True]
            P.op("dve", lambda e: e.memset(u_buf[:, 0:2], 0.0), w=["u0"], war=RYK)
            cw = lambda tap, m: GT[:, m, 24 + 3 * j + tap:24 + 3 * j + tap + 1]
            gi = 0
            first_y = {t: True for t in ALLT}
            for m0 in range(0, KC, 2):
                srcs = [win[:, sec * D + m0 * 128: sec * D + (m0 + 2) * 128].rearrange("(k p) c -> p k c", p=128)
                        for sec in range(3)]
                s, wv = wfill([(sec * 2048, KC, 256, srcs[sec]) for sec in range(3)])
                for ml in range(2):
                    m = m0 + ml
                    for (t, c0) in all_tiles:
                        n = TILES[t][1]
                        if t == 3 and pend_B[0]:
                            drain(micro_post)
                            pre_norm_mixer_B(vpre)
                            pend_B[0] = False
                        if t == 0:
                            pbs = [7, 7, 7]
                            pvs = [banks[7][:, 16 * sec:16 * sec + n] for sec in range(3)]
                        else:
                            pbs = [3 * (gi % 2), 3 * (gi % 2) + 1, 3 * (gi % 2) + 2]
                            pvs = [banks[q][:, 0:n] for q in pbs]
                            gi += 1
                        for sec in range(3):
                            mm_group(pvs[sec],
                                     [(wv[sec][:, k, ml * 128:(ml + 1) * 128], hnv(c0, n)[:, k, :]) for k in range(KC)],
                                     r=[("ring", s), hnkey(t)], w=[("pb", pbs[sec])])
                        if pend_B[0]:
                            drain(micro_post, 10)
                        bp, cp, xp = pvs
                        xs = sgt[gi % 2]
                        P.op("act", lambda e, xs=xs, xp=xp, n=n: e.activation(out=xs[:, 0:n], in_=xp[:, 0:n], func=AF.Copy),
                             r=[("pb", pbs[2])], w=[("sg", 2 * (gi % 2))])
                        P.op("dve", lambda e, xs=xs, cp=cp, n=n, c0=c0: e.tensor_tensor(
                            out=u_buf[:, 2 + c0:2 + c0 + n], in0=xs[:, 0:n], in1=cp[:, 0:n], op=ALU.mult),
                            r=[("sg", 2 * (gi % 2)), ("pb", pbs[1]), "u0"], w=["u"])
                        P.op("dve", lambda e, m=m, n=n, c0=c0: e.tensor_scalar(
                            out=vtmp[:, 0:n], in0=u_buf[:, c0:c0 + n], scalar1=cw(0, m), scalar2=None, op0=ALU.mult),
                            r=["u", "GT"], w=["vtmp"])
                        P.op("dve", lambda e, m=m, n=n, c0=c0: e.scalar_tensor_tensor(
                            out=vtmp[:, 0:n], in0=u_buf[:, 1 + c0:1 + c0 + n], scalar=cw(1, m), in1=vtmp[:, 0:n],
                            op0=ALU.mult, op1=ALU.add), r=["u", "vtmp"], w=["vtmp"])
                        P.op("dve", lambda e, m=m, n=n, c0=c0: e.scalar_tensor_tensor(
                            out=vtmp[:, 0:n], in0=u_buf[:, 2 + c0:2 + c0 + n], scalar=cw(2, m), in1=vtmp[:, 0:n],
                            op0=ALU.mult, op1=ALU.add), r=["u", "vtmp"], w=["vtmp"])
                        P.op("dve", lambda e, m=m, n=n, c0=c0, bp=bp: e.tensor_tensor(
                            out=y_all[:, m, c0:c0 + n], in0=vtmp[:, 0:n], in1=bp[:, 0:n], op=ALU.mult),
                            r=["vtmp", ("pb", pbs[0])], w=[("y", t)], war=(RYK if first_y[t] else ()))
                        first_y[t] = False
            mixer_out(cout_d[j], vpost, "y", nxt)

        def attn_mixer(j, li_idx, vpre, vpost, nxt):
            wqkv = qkv_d[j]
            neglam = lamt[:, 4 + j:5 + j]
            sgain = GT[:, 0, 30 + j:31 + j]
            mixer_pre(vpre)
            drain(micro_post)
            pre_norm_mixer_B(vpre)
            barrier()
            P.op("sp", lambda e: e.dma_start(out=rope_sb, in_=rope_d),
                 r=[("bar", "pe"), ("bar", "act"), ("bar", "dve")], w=["rope"], war=FSK, dsem="rp")
            gstep = [0]
            carry = [None]
            first_y = {t: True for t in ALLT}
            for hp in range(NH // 2):
                srcs = [wqkv[:, sec * D + hp * 256: sec * D + (hp + 1) * 256].rearrange("(k p) c -> p k c", p=128)
                        for sec in range(3)]
                s, wv = wfill([(sec * 2048, KC, 256, srcs[sec]) for sec in range(3)])
                for hl in range(2):
                    hd = 2 * hp + hl

                    porder = [1, 2, 3, 4, 0]

                    def proj(ti):
                        t, c0 = all_tiles[porder[ti]]
                        n = TILES[t][1]
                        par = (ti + 1) % 2
                        for sec in range(2):
                            pb = 2 * par + sec
                            mm_group(banks[pb][:, 0:n],
                                     [(wv[sec][:, k, hl * 128:(hl + 1) * 128], hnv(c0, n)[:, k, :]) for k in range(KC)],
                                     r=[("ring", s), hnkey(t)], w=[("pb", pb)])
                            tb = (tq, tk)[sec][par]
                            P.op("act", lambda e, pb=pb, n=n, tb=tb: e.activation(out=tb[:, 0:n], in_=banks[pb][:, 0:n],
                                                                                  func=AF.Copy),
                                 r=[("pb", pb)], w=[("tqk", sec, par)])

                    def rot(ti):
                        t, c0 = all_tiles[porder[ti]]
                        n = TILES[t][1]
                        par = (ti + 1) % 2
                        for sec, dst in ((0, qr), (1, kr)):
                            pb = 4 + 2 * par + sec
                            tb = (tq, tk)[sec][par]
                            P.op("pe", lambda e, pb=pb, n=n, tb=tb: e.matmul(banks[pb][:, 0:n], rmat, tb[:, 0:n],
                                                                             start=True, stop=True),
                                 r=[("tqk", sec, par), "cst"], w=[("pb", pb)])
                            P.op("dve", lambda e, n=n, tb=tb, c0=c0: e.tensor_tensor(
                                out=tb[:, 0:n], in0=tb[:, 0:n], in1=rope_sb[:, 0, c0:c0 + n], op=ALU.mult),
                                r=[("tqk", sec, par), "rope"], w=[("tqk", sec, par)])
                            P.op("dve", lambda e, pb=pb, n=n, c0=c0: e.tensor_tensor(
                                out=banks[pb][:, 0:n], in0=banks[pb][:, 0:n], in1=rope_sb[:, 1, c0:c0 + n], op=ALU.mult),
                                r=[("pb", pb), "rope"], w=[("pb", pb)])
                            P.op("dve", lambda e, pb=pb, n=n, c0=c0, dst=dst, tb=tb: e.tensor_tensor(
                                out=dst[:, c0:c0 + n], in0=tb[:, 0:n], in1=banks[pb][:, 0:n], op=ALU.add),
                                r=[("tqk", sec, par), ("pb", pb)], w=[("qk", sec)], war=(RYK if ti == 0 and hd == 0 else ()))

                    proj(0)
                    if carry[0] is not None:
                        carry[0]()
                        carry[0] = None
                    for ti in range(5):
                        if ti + 1 < 5:
                            proj(ti + 1)
                        rot(ti)
                    for g0 in range(0, 17, 4):
                        kts = list(range(g0, min(17, g0 + 4)))
                        pb = 0 if (g0 // 4) % 2 == 0 else 2
                        for ii, kt in enumerate(kts):
                            kc0, kn = KT[kt]
                            mm_group(banks[pb][0:kn, ii * 128:(ii + 1) * 128],
                                     [(hnv(kc0, kn)[:, k, :], wv[2][:, k, hl * 128:(hl + 1) * 128]) for k in range(KC)],
                                     r=[("ring", s)] + [hnkey(t_) for t_ in ALLT], w=[("pb", pb)])
                        for ii, kt in enumerate(kts):
                            kn = KT[kt][1]
                            P.op("act", lambda e, pb=pb, ii=ii, kt=kt, kn=kn: e.activation(
                                out=vt[0:kn, kt, :], in_=banks[pb][0:kn, ii * 128:(ii + 1) * 128], func=AF.Copy),
                                r=[("pb", pb)], w=["vt"], war=(RYK if hd == 0 and kt == 0 else ()))
                    steps = []
                    for qt in range(5):
                        qc0, qn = TILES[qt]
                        if qt == 0:
                            st = [(0, 0, qn, True)]
                        else:
                            st = [(0, 0, qn, False)]
                            st += [(1 + kt, 0, qn, False) for kt in range(4 * (qt - 1))]
                            st += [(1 + 4 * (qt - 1) + r_, 128 * r_, qn - 128 * r_, True) for r_ in range(4)]
                        for si, (kt, lo, wd, diag) in enumerate(st):
                            steps.append((qt, si, len(st), kt, lo, wd, diag))

                    def scores(gi_, step):
                        qt, si, nst, kt, lo, wd, diag = step
                        qc0, qn = TILES[qt]
                        kc0, kn = KT[kt]
                        sb_ = gi_ % 2
                        for c in range(2):
                            pb = 2 * sb_ + c
                            P.op("pe", lambda e, pb=pb, c=c, kc0=kc0, kn=kn, lo=lo, wd=wd, qc0=qc0: e.matmul(
                                banks[pb][0:kn, 0:wd], kr[c * 64:(c + 1) * 64, kc0:kc0 + kn],
                                qr[c * 64:(c + 1) * 64, qc0 + lo:qc0 + lo + wd], start=True, stop=True),
                                r=[("qk", 0), ("qk", 1)], w=[("pb", pb)])
                            pt = pT[pb]
                            P.op("act", lambda e, pb=pb, pt=pt, kn=kn, wd=wd: e.activation(
                                out=pt[0:kn, 0:wd], in_=banks[pb][0:kn, 0:wd], func=AF.Exp, scale=HD ** -0.5),
                                r=[("pb", pb)], w=[("pT", pb)])
                            if diag:
                                dn = min(kn, wd)
                                P.op("dve", lambda e, pt=pt, kn=kn, dn=dn: e.tensor_tensor(
                                    out=pt[0:kn, 0:dn], in0=pt[0:kn, 0:dn], in1=trib[0:kn, 0:dn], op=ALU.mult),
                                    r=[("pT", pb), "trib"], w=[("pT", pb)])

                    def pv(gi_, step):
                        qt, si, nst, kt, lo, wd, diag = step
                        kc0, kn = KT[kt]
                        sb_ = gi_ % 2
                        for c in range(2):
                            pb = 2 * sb_ + c
                            pt = pT[pb]
                            P.op("pe", lambda e, c=c, pt=pt, kt=kt, kn=kn, lo=lo, wd=wd, si=si, nst=nst: e.matmul(
                                banks[4 + c][:, lo:lo + wd], vt[0:kn, kt, :], pt[0:kn, 0:wd],
                                start=(si == 0), stop=(si == nst - 1)),
                                r=[("pT", pb), "vt"], w=[("pb", 4 + c)])
                            P.op("pe", lambda e, c=c, pt=pt, kn=kn, lo=lo, wd=wd, si=si, nst=nst: e.matmul(
                                banks[6 + c][:, lo:lo + wd], onesb[0:kn, :], pt[0:kn, 0:wd],
                                start=(si == 0), stop=(si == nst - 1)),
                                r=[("pT", pb), "onesb"], w=[("pb", 6 + c)])

                    def finalize(gi_, qt):
                        qc0, qn = TILES[qt]
                        k0, k1 = ("tqk", 0, 0), ("tqk", 0, 1)
                        o0, o1 = tk[0], tk[1]
                        ko0, ko1 = ("tqk", 1, 0), ("tqk", 1, 1)
                        P.op("act", lambda e, qn=qn: e.activation(out=o0[:, 0:qn], in_=banks[4][:, 0:qn], func=AF.Copy),
                             r=[("pb", 4)], w=[ko0])
                        P.op("dve", lambda e, qn=qn: e.tensor_copy(out=o1[:, 0:qn], in_=banks[5][:, 0:qn]),
                             r=[("pb", 5)], w=[ko1])
                        P.op("act", lambda e, qn=qn: e.activation(out=t1[:, 0:qn], in_=banks[7][:, 0:qn], func=AF.Copy),
                             r=[("pb", 7)], w=[k1])
                        P.op("dve", lambda e, qn=qn: e.tensor_copy(out=t0[:, 0:qn], in_=banks[6][:, 0:qn]),
                             r=[("pb", 6)], w=[k0])
                        P.op("dve", lambda e, qn=qn: e.tensor_tensor(out=o0[:, 0:qn], in0=o0[:, 0:qn], in1=t1[:, 0:qn],
                                                                     op=ALU.mult), r=[k1, ko0], w=[ko0])
                        P.op("dve", lambda e, qn=qn: e.tensor_tensor(out=o1[:, 0:qn], in0=o1[:, 0:qn], in1=t0[:, 0:qn],
                                                                     op=ALU.mult), r=[k0, ko1], w=[ko1])
                        P.op("dve", lambda e, qn=qn: e.scalar_tensor_tensor(out=o0[:, 0:qn], in0=o1[:, 0:qn], scalar=neglam,
                                                                            in1=o0[:, 0:qn], op0=ALU.mult, op1=ALU.add),
                             r=[ko0, ko1, "lamt"], w=[ko0])
                        P.op("act", lambda e, qn=qn: e.activation(out=osq[:, 0:qn], in_=o0[:, 0:qn], func=AF.Square),
                             r=[ko0], w=["osq"])
                        P.op("dve", lambda e, qn=qn: e.tensor_tensor(out=t0[:, 0:qn], in0=t0[:, 0:qn], in1=t1[:, 0:qn],
                                                                     op=ALU.mult), r=[k0, k1], w=[k0])
                        P.op("dve", lambda e, qn=qn: e.scalar_tensor_tensor(out=t0[:, 0:qn], in0=t0[:, 0:qn],
                                                                            scalar=128.0 * SUBLN_EPS, in1=t0[:, 0:qn],
                                                                            op0=ALU.mult, op1=ALU.mult),
                             r=[k0], w=[k0])

                    def finalize2(gi_, qt, hd=hd, sbk=None):
                        qc0, qn = TILES[qt]
                        k0 = ("tqk", 0, 0)
                        o0 = tk[0]
                        ko0 = ("tqk", 1, 0)
                        if sbk is None:
                            sbk = 2 * (gi_ % 2)
                        P.op("pe", lambda e, qn=qn, sbk=sbk: e.matmul(banks[sbk][:, 0:qn], onesb[:, :], osq[:, 0:qn],
                                                                      start=True, stop=True),
                             r=["osq", "onesb"], w=[("pb", sbk)])
                        P.op("dve", lambda e, qn=qn, sbk=sbk: e.tensor_tensor(out=t0[:, 0:qn], in0=t0[:, 0:qn],
                                                                              in1=banks[sbk][:, 0:qn], op=ALU.add),
                             r=[k0, ("pb", sbk)], w=[k0])
                        P.op("act", lambda e, qn=qn: e.activation(out=rstd_a[:, 0:qn], in_=t0[:, 0:qn],
                                                                  func=AF.Ln, scale=2.0 ** -20),
                             r=[k0], w=["rstd_a"])
                        P.op("act", lambda e, qn=qn: e.activation(out=rstd_a[:, 0:qn], in_=rstd_a[:, 0:qn],
                                                                  func=AF.Exp, scale=-0.5, bias=-10.0 * math.log(2.0)),
                             r=["rstd_a"], w=["rstd_a"])
                        P.op("dve", lambda e, qn=qn, qc0=qc0, hd=hd: e.scalar_tensor_tensor(
                            out=y_all[:, hd, qc0:qc0 + qn], in0=o0[:, 0:qn], scalar=sgain, in1=rstd_a[:, 0:qn],
                            op0=ALU.mult, op1=ALU.mult), r=[ko0, "rstd_a", "GT"], w=[("y", qt)],
                            war=(RYK if first_y[qt] else ()))
                        first_y[qt] = False

                    g0_ = gstep[0]
                    scores(g0_, steps[0])
                    pend = None
                    for i, step in enumerate(steps):
                        if i + 1 < len(steps):
                            scores(g0_ + i + 1, steps[i + 1])
                        pv(g0_ + i, step)
                        if pend is not None and i >= pend[0]:
                            finalize2(pend[1], pend[2])
                            pend = None
                        if step[1] == step[2] - 1:
                            finalize(g0_ + i, step[0])
                            pend = (i + 4, g0_ + i + 4, step[0])
                    if pend is not None:
                        carry[0] = (lambda f2=finalize2, a=pend[1], b=pend[2]: f2(a, b, sbk=6))
                    gstep[0] += len(steps)
            if carry[0] is not None:
                carry[0]()
                carry[0] = None
            barrier()
            if dbg:
                P.op("pool", lambda e: e.dma_start(out=dbg_d, in_=RYb[:, 0:KC * NT]), r=[("y", t) for t in range(5)], dsem="dbg")
                P.op("pool", lambda e: e.nop(), w=[("y", t) for t in range(5)])
            mixer_out(wo_d[j], vpost, "y", nxt)

        steps = []
        for i in range(depth):
            steps.append(("ffn1", i))
            steps.append(("mix", i))
            steps.append(("ffn2", i))
        if nsteps is not None:
            steps = steps[:nsteps]
        items = []
        for kind, i in steps:
            if kind == "ffn1":
                items += [("ffn", gu1_d[i], dn1_d[i], 2 * i, 2 * i + 1, S) for S in SUPERS]
            elif kind == "ffn2":
                items += [("ffn", gu2_d[i], dn2_d[i], 16 + 2 * i, 16 + 2 * i + 1, S) for S in SUPERS]
            else:
                items.append(("mix", i))
        for ii, item in enumerate(items):
            nxt = items[ii + 1] if ii + 1 < len(items) else None
            if item[0] == "ffn":
                ffn_pass(item, nxt)
            else:
                i = item[1]
                if i % 2 == 0:
                    conv_mixer(i // 2, 8 + 2 * i, 8 + 2 * i + 1, nxt)
                else:
                    attn_mixer(i // 2, i, 8 + 2 * i, 8 + 2 * i + 1, nxt)
        flush_deferred()

        barrier()
        ostg = [RYf[:, i * D:(i + 1) * D] for i in range(4)]
        for bi in range(SEQ // 128):
            c0 = NMETA + 128 * bi
            s = bi % 4
            tq = 1 + bi // 4
            for half in range(2):
                bk = (2 * bi + half) % 4
                for kk in range(4):
                    k = 4 * half + kk
                    P.op("pe", lambda e, k=k, kk=kk, bk=bk, c0=c0: e.transpose(
                        banks[bk][:, kk * 128:(kk + 1) * 128], h[:, k, c0:c0 + 128], ident),
                        r=[hkey(tq), "cst"], w=[("pb", bk)])
                dst = ostg[s][:, 512 * half:512 * (half + 1)]
                if half == 0:
                    P.op("act", lambda e, dst=dst, bk=bk: e.activation(out=dst, in_=banks[bk][:, :], func=AF.Copy),
                         r=[("pb", bk)], w=[("ostg", s)])
                else:
                    P.op("dve", lambda e, dst=dst, bk=bk: e.tensor_copy(out=dst, in_=banks[bk][:, :]),
                         r=[("pb", bk)], w=[("ostg", s)])
            P.op("sp", lambda e, s=s, bi=bi: e.dma_start(out=out_d[bi * 128:(bi + 1) * 128, :], in_=ostg[s]),
                 r=[("ostg", s)], dsem=f"st{s}")
        for s in range(4):
            P.op("sp", lambda e: e.nop(), w=[("ostg", s)])

        P.finalize()
        sems = {e: es.enter_context(nc.semaphore(f"s_{e}")) for e in ENGS}
        dsems = {n: es.enter_context(nc.semaphore(f"d_{n}")) for n in P.dma_cnt}
        with nc.Block() as block:
            @block.sync
            def _(e):
                P.replay("sp", e, sems, dsems)

            @block.scalar
            def _(e):
                P.replay("act", e, sems, dsems)

            @block.vector
            def _(e):
                P.replay("dve", e, sems, dsems)

            @block.gpsimd
            def _(e):
                P.replay("pool", e, sems, dsems)

            @block.tensor
            def _(e):
                P.replay("pe", e, sems, dsems)
    return nc


def _consts():
    ident = np.eye(128, dtype=np.float32)
    rmat = np.zeros((128, 128), np.float32)
    for c in range(2):
        for d in range(8):
            rmat[c * 64 + d + 8, c * 64 + d] = -1.0
            rmat[c * 64 + d, c * 64 + d + 8] = 1.0
    tri = (np.arange(128)[None, :] >= np.arange(128)[:, None]).astype(np.float32)
    pos = np.arange(NT, dtype=np.float32)
    inv_freq = (np.float32(THETA) ** (-np.arange(0, ROT, 2, dtype=np.float32) / np.float32(ROT))).astype(np.float32)
    ang = (pos[:, None] * inv_freq[None, :]).astype(np.float32)
    cos = np.cos(ang).astype(np.float32)
    sin = np.sin(ang).astype(np.float32)
    rope = np.zeros((128, 2, NT), np.float32)
    rope[:, 0, :] = 1.0
    for c in range(2):
        for d in range(16):
            rope[c * 64 + d, 0, :] = cos[:, d % 8]
            rope[c * 64 + d, 1, :] = sin[:, d % 8]
    return np.concatenate([ident, rmat], axis=1), tri, rope


_CACHE = {}


def kernel(x, meta_tokens, ln_ffn1, ffn1_w_gu, ffn1_w_down, ln_mix, conv_w_in, conv_w, conv_w_out,
           attn_w_qkv, attn_lambda, attn_subln_g, attn_w_o, ln_ffn2, ffn2_w_gu, ffn2_w_down,
           _depth=DEPTH, _nsteps=None, _dbg=False):
    f = lambda a: np.ascontiguousarray(np.asarray(a, dtype=np.float32))
    x = f(x)
    nb = x.shape[0]
    vecs = np.zeros((32, D), np.float32)
    vecs[0:8] = f(ln_ffn1).reshape(8, D)
    vecs[8:16] = f(ln_mix).reshape(8, D)
    vecs[16:24] = f(ln_ffn2).reshape(8, D)
    vecs[24:30] = f(conv_w).reshape(6, D)
    vecs[30:32, 0:128] = f(attn_subln_g)
    cst, tri, rope = _consts()
    shared = {
        "meta": f(meta_tokens), "vecs": vecs, "lam": f(attn_lambda).reshape(1, 512),
        "ffn1_w_gu": f(ffn1_w_gu), "ffn1_w_down": f(ffn1_w_down),
        "ffn2_w_gu": f(ffn2_w_gu), "ffn2_w_down": f(ffn2_w_down),
        "conv_w_in": f(conv_w_in), "conv_w_out": f(conv_w_out),
        "attn_w_qkv": f(attn_w_qkv), "attn_w_o": f(attn_w_o),
        "rope": rope, "consts": cst, "tri": tri,
    }
    key = (_depth, _nsteps, _dbg)
    if key not in _CACHE:
        _CACHE[key] = build_program(_depth, _nsteps, _dbg)
    nc = _CACHE[key]
    in_maps = [dict(shared, x=x[b]) for b in range(nb)]
    res = run_bass_kernel_spmd(nc, in_maps, core_ids=list(range(nb)))
    if _dbg:
        return np.stack([r["out"] for r in res.results], axis=0).astype(np.float32), res.results[0]["dbg"]
    return np.stack([r["out"] for r in res.results], axis=0).astype(np.float32)
```

```python
import math
from contextlib import ExitStack

import numpy as np
import concourse.bass as bass
import concourse.mybir as mybir
from concourse.bass_utils import run_bass_kernel_spmd

F32 = mybir.dt.float32
BF16 = mybir.dt.bfloat16
ALU = mybir.AluOpType
AF = mybir.ActivationFunctionType

D = 1024
FF = 2816
SEQ = 2048
NMETA = 16
NT = SEQ + NMETA
DEPTH = 4
KC = D // 128
JC = FF // 128
HD = 64
NH = 8
ROT = 16
THETA = 500000.0
NORM_EPS = 1e-6
SUBLN_EPS = 1e-5

TILES = [(0, 16), (16, 512), (528, 512), (1040, 512), (1552, 512)]
SUPERS = [[0, 1, 2], [3, 4]]
STW = 1040
KT = [(0, 16)] + [(16 + 128 * i, 128) for i in range(16)]

ENGS = ("pe", "act", "dve", "pool", "sp")
NSLOT = 2
SLOT_EL = 8192


class _Op:
    __slots__ = ("eng", "fn", "deps", "signal", "semval", "dsem", "dval", "idx", "ndma")

    def __init__(self, eng, fn):
        self.eng = eng
        self.fn = fn
        self.deps = []
        self.signal = False
        self.semval = 0
        self.dsem = None
        self.dval = 0
        self.idx = 0
        self.ndma = 0


class Prog:
    def __init__(self):
        self.ops = {e: [] for e in ENGS}
        self.res = {}
        self.waited = {e: {} for e in ENGS}
        self.dma_cnt = {}
        self.capture = None

    def _src(self, op):
        if op.dsem is not None:
            return ("d", op.dsem), op.dval
        return ("e", op.eng), op.idx

    def _add_dep(self, op, dep):
        if dep is None or dep is op:
            return
        if dep.dsem is None and dep.eng == "pe" and op.eng == "pe" and op.dsem is None:
            return
        src, v = self._src(dep)
        w = self.waited[op.eng]
        if w.get(src, -1) >= v:
            return
        w[src] = v
        op.deps.append(dep)
        if dep.dsem is None:
            dep.signal = True

    def op(self, *a, **k):
        if self.capture is not None:
            self.capture.append((a, k))
            return None
        return self._op(*a, **k)

    def _op(self, eng, fn, r=(), w=(), dsem=None, ndma=1, war=()):
        o = _Op(eng, fn)
        lst = self.ops[eng]
        o.idx = len(lst)
        if dsem is not None:
            o.dsem = dsem
            o.ndma = ndma
            self.dma_cnt[dsem] = self.dma_cnt.get(dsem, 0) + ndma
            o.dval = 16 * self.dma_cnt[dsem]
        for k in r:
            e = self.res.get(k)
            if e is not None:
                self._add_dep(o, e[0])
        for k in tuple(w) + tuple(war):
            e = self.res.get(k)
            if e is not None:
                self._add_dep(o, e[0])
                for rd in e[1]:
                    self._add_dep(o, rd)
        for k in r:
            e = self.res.get(k)
            if e is None:
                self.res[k] = [None, [o]]
            else:
                e[1].append(o)
        for k in w:
            self.res[k] = [o, []]
        lst.append(o)
        return o

    def finalize(self):
        for e in ENGS:
            c = 0
            for o in self.ops[e]:
                if o.dsem is None and o.signal:
                    c += 1
                    o.semval = c

    def replay(self, eng, engine, sems, dsems):
        for o in self.ops[eng]:
            for d in o.deps:
                if d.dsem is not None:
                    engine.wait_ge(dsems[d.dsem], d.dval)
                else:
                    engine.wait_ge(sems[d.eng], d.semval)
            ins = o.fn(engine)
            if o.dsem is not None:
                if not isinstance(ins, (list, tuple)):
                    ins = [ins]
                assert len(ins) == o.ndma
                for i in ins:
                    i.then_inc(dsems[o.dsem], 16)
            elif o.signal:
                ins.then_inc(sems[o.eng], 1)


def lambda_init(i):
    return 0.8 - 0.6 * math.exp(-0.3 * i)


def build_program(depth=DEPTH, nsteps=None, dbg=False):
    nc = bass.Bass("TRN2", target_bir_lowering=False)
    P = Prog()

    def din(name, shape):
        return nc.dram_tensor(name, list(shape), F32, kind="ExternalInput").ap()

    x_d = din("x", (SEQ, D))
    meta_d = din("meta", (NMETA, D))
    vec_d = din("vecs", (32, D))
    lam_d = din("lam", (1, 512))
    gu1_d = din("ffn1_w_gu", (DEPTH, D, 2 * FF))
    dn1_d = din("ffn1_w_down", (DEPTH, FF, D))
    gu2_d = din("ffn2_w_gu", (DEPTH, D, 2 * FF))
    dn2_d = din("ffn2_w_down", (DEPTH, FF, D))
    cin_d = din("conv_w_in", (2, D, 3 * D))
    cout_d = din("conv_w_out", (2, D, D))
    qkv_d = din("attn_w_qkv", (2, D, 3 * D))
    wo_d = din("attn_w_o", (2, D, D))
    rope_d = din("rope", (128, 2, NT))
    cst_d = din("consts", (128, 256))
    tri_d = din("tri", (128, 128))
    out_d = nc.dram_tensor("out", [SEQ, D], F32, kind="ExternalOutput").ap()
    dbg_d = nc.dram_tensor("dbg", [128, KC * NT], F32, kind="ExternalOutput").ap() if dbg else None

    es = ExitStack()
    with es:
        sb = lambda n, s, d: es.enter_context(nc.sbuf_tensor(n, list(s), d))
        h = sb("h", (128, KC, NT), F32)
        ring = sb("ring", (128, NSLOT, SLOT_EL), BF16)
        RX = sb("RX", (128, 3 * KC * STW), BF16)
        RY = sb("RY", (128, JC * STW), BF16)
        MISC = sb("MISC", (128, 3840), F32)
        GT = sb("GT", (128, KC, 32), F32)
        cst = sb("cst", (128, 256), F32)
        onesb = sb("onesb", (128, 128), BF16)
        trib = sb("trib", (128, 128), BF16)
        lamt = sb("lamt", (128, 16), F32)
        banks = [es.enter_context(nc.psum_tensor(f"pb{i}", [128, 512], F32)) for i in range(8)]

        ident = cst[:, 0:128]
        rmat = cst[:, 128:256]

        RXb = RX[:]
        RXf = RX[:].bitcast(F32)
        RYb = RY[:]
        RYf = RY[:].bitcast(F32)
        MF = MISC[:]
        MB = MISC[:].bitcast(BF16)

        hn_st = RXb[:, 0:KC * STW].rearrange("p (k c) -> p k c", k=KC)
        f_st = RXb[:, KC * STW:3 * KC * STW].bitcast(F32).rearrange("p (k c) -> p k c", k=KC)
        hn_B = RXb[:, KC * STW:KC * STW + KC * 1024].rearrange("p (k c) -> p k c", k=KC)

        def hnv(c0, n):
            if c0 < STW:
                assert c0 + n <= STW
                return hn_st[:, :, c0:c0 + n]
            return hn_B[:, :, c0 - STW:c0 - STW + n]

        def hnkey(t):
            return ("hn", t) if t < 3 else ("hnm", t)
        rope_sb = RXb[:, KC * NT:KC * NT + 4 * NT].bitcast(F32).rearrange("p (a c) -> p a c", a=2)
        act = RYb.rearrange("p (j c) -> p j c", j=JC)
        y_all = RYb[:, 0:KC * NT].rearrange("p (k c) -> p k c", k=KC)
        rstd = [MF[:, 0:512], MF[:, 512:1024]]
        sgt = [MF[:, 1024:1536], MF[:, 1536:2048]]
        sqb = [MB[:, 4096 + 512 * i: 4096 + 512 * (i + 1)] for i in range(3)]
        vtmp = MF[:, 2816:3328]
        pT = [MB[:, 512 * i: 512 * (i + 1)] for i in range(4)]
        tq = [MF[:, 1024:1536], MF[:, 1536:2048]]
        tk = [MF[:, 2048:2560], MF[:, 2560:3072]]
        t0 = tq[0]
        t1 = tq[1]
        osq = MB[:, 6144:6656]
        rstd_a = MF[:, 3328:3840]
        u_buf = RYf[:, KC * NT // 2: KC * NT // 2 + NT + 2]
        qr = RYb[:, KC * NT: KC * NT + NT]
        kr = RYb[:, KC * NT + NT: KC * NT + 2 * NT]
        vt = RYb[:, KC * NT + 2 * NT: KC * NT + 2 * NT + 17 * 128].rearrange("p (t v) -> p t v", t=17)

        def slot_view(s, off, a, b):
            return ring[:, s, off:off + a * b].rearrange("p (a b) -> p a b", a=a)

        bar_cnt = [0]

        def barrier():
            i = bar_cnt[0]
            bar_cnt[0] += 1
            P.op("act", lambda e: e.activation(out=lamt[:, 8:9], in_=lamt[:, 8:9], func=AF.Copy),
                 w=[("bar", "act")])
            P.op("dve", lambda e: e.tensor_copy(out=lamt[:, 9:10], in_=lamt[:, 9:10]), w=[("bar", "dve")])
            P.op("pe", lambda e: e.matmul(banks[7][0:1, 0:2], ident[0:1, 0:1], ident[0:1, 0:2],
                                          start=True, stop=True),
                 r=[("bar", "act"), ("bar", "dve")], w=[("bar", "pe"), ("pb", 7)])
            P.op("act", lambda e: e.activation(out=lamt[:, 8:9], in_=lamt[:, 8:9], func=AF.Copy),
                 r=[("bar", "pe"), ("bar", "dve")], w=[("bar", "act")])
            P.op("dve", lambda e: e.tensor_copy(out=lamt[:, 9:10], in_=lamt[:, 9:10]),
                 r=[("bar", "pe"), ("bar", "act")], w=[("bar", "dve")])

        def mm_group(out_ps, pairs, r, w):
            def fn(e):
                n = len(pairs)
                ins = None
                for i, (l, rh) in enumerate(pairs):
                    ins = e.matmul(out_ps, l, rh, start=(i == 0), stop=(i == n - 1))
                return ins
            return P.op("pe", fn, r=r, w=w)

        slot_ctr = [0]

        def wfill(dmas):
            s = slot_ctr[0] % NSLOT
            slot_ctr[0] += 1
            views, srcs = [], []
            for (off, a, b, src) in dmas:
                v = slot_view(s, off, a, b)
                for a0 in range(0, a, 4):
                    a1 = min(a, a0 + 4)
                    views.append(v[:, a0:a1, :])
                    srcs.append(src[:, a0:a1, :])
            full_views = [slot_view(s, off, a, b) for (off, a, b, _) in dmas]

            def fn(e):
                return [e.dma_start(out=v, in_=sr) for v, sr in zip(views, srcs)]
            P.op("pool", fn, w=[("ring", s)], dsem=f"ws{s}", ndma=len(views))
            return s, full_views

        def norm_stats(src3, n, rs, rkeys, eps_total, kc=KC, bank=6):
            def sq(k):
                b = sqb[k % 3]
                P.op("act", lambda e, k=k, b=b: e.activation(out=b[:, 0:n], in_=src3[:, k, :], func=AF.Square),
                     r=rkeys, w=[("sqb", k % 3)])

            def mm(k):
                b = sqb[k % 3]
                P.op("pe", lambda e, k=k, b=b: e.matmul(banks[bank][:, 0:n], onesb[:, :], b[:, 0:n],
                                                         start=(k == 0), stop=(k == kc - 1)),
                     r=[("sqb", k % 3)], w=[("pb", bank)])
            sq(0)
            sq(1)
            for k in range(2, kc):
                sq(k)
                mm(k - 2)
            mm(kc - 2)
            mm(kc - 1)
            P.op("act", lambda e: e.activation(out=rs[:, 0:n], in_=banks[bank][:, 0:n], func=AF.Ln,
                                               bias=eps_total, scale=1.0),
                 r=[("pb", bank)], w=[("rstd", id(rs))])
            P.op("act", lambda e: e.activation(out=rs[:, 0:n], in_=rs[:, 0:n], func=AF.Exp, scale=-0.5),
                 r=[("rstd", id(rs))], w=[("rstd", id(rs))])

        def gvec(v, k):
            return GT[:, k, v:v + 1]

        P.op("sp", lambda e: e.dma_start(out=cst[:, :], in_=cst_d), w=["cst"], dsem="c0")
        P.op("dve", lambda e: e.memset(onesb[:, :], 1.0), w=["onesb"])
        P.op("dve", lambda e: e.memset(lamt[:, :], 0.0), w=["lamt"])
        P.op("pool", lambda e: e.dma_start(out=trib[:, :], in_=tri_d), w=["trib"], dsem="c3")
        vst = RXf[0:32, 0:D]
        P.op("sp", lambda e: e.dma_start(out=vst, in_=vec_d), w=["vst"], dsem="c1")
        for k in range(KC):
            P.op("pe", lambda e, k=k: e.transpose(banks[0][:, k * 32:(k + 1) * 32], vst[:, k * 128:(k + 1) * 128],
                                                  ident[0:32, 0:32]),
                 r=["vst", "cst"], w=[("pb", 0)])
        P.op("act", lambda e: e.activation(out=GT[:, :, :],
                                           in_=banks[0][:, 0:KC * 32].rearrange("p (k r) -> p k r", k=KC),
                                           func=AF.Copy),
             r=[("pb", 0)], w=["GT"])
        sD = math.sqrt(D)
        P.op("dve", lambda e: e.tensor_scalar(out=GT[:, :, 0:24], in0=GT[:, :, 0:24], scalar1=sD, scalar2=None,
                                              op0=ALU.mult), r=["GT"], w=["GT"])
        for lo in (1, 17):
            P.op("dve", lambda e, lo=lo: e.tensor_scalar(out=GT[:, :, lo:lo + 7:2], in0=GT[:, :, lo:lo + 7:2],
                                                         scalar1=0.5, scalar2=None, op0=ALU.mult),
                 r=["GT"], w=["GT"])
        lamw = MF[:, 0:512]
        P.op("sp", lambda e: e.dma_start(out=lamw, in_=lam_d.rearrange("a b -> (a b)").partition_broadcast(128)),
             w=["lamw"], dsem="c2")
        lam4 = lamw.rearrange("p (a b c) -> p a b c", a=2, b=4)
        for a in range(2):
            for q in range(2):
                P.op("dve", lambda e, a=a, q=q: e.tensor_tensor(out=MF[:, 512:576], in0=lam4[:, a, 2 * q, :],
                                                                in1=lam4[:, a, 2 * q + 1, :], op=ALU.mult),
                     r=["lamw"], w=["lamp"])
                P.op("dve", lambda e, a=a, q=q: e.reduce_sum(out=lamt[:, 2 * a + q:2 * a + q + 1], in_=MF[:, 512:576],
                                                             axis=mybir.AxisListType.X),
                     r=["lamp"], w=["lamt"])
        P.op("act", lambda e: e.activation(out=lamt[:, 0:4], in_=lamt[:, 0:4], func=AF.Exp), r=["lamt"], w=["lamt"])
        for a in range(2):
            li = lambda_init(2 * a + 1)
            P.op("dve", lambda e, a=a, li=li: e.scalar_tensor_tensor(out=lamt[:, 4 + a:5 + a], in0=lamt[:, 2 * a + 1:2 * a + 2],
                                                                     scalar=-li, in1=lamt[:, 2 * a:2 * a + 1],
                                                                     op0=ALU.add, op1=ALU.subtract),
                 r=["lamt"], w=["lamt"])
            P.op("dve", lambda e, a=a, li=li: e.tensor_scalar(out=GT[:, 0:1, 30 + a:31 + a], in0=GT[:, 0:1, 30 + a:31 + a],
                                                              scalar1=math.sqrt(128.0) * (1.0 - li), scalar2=None,
                                                              op0=ALU.mult), r=["GT"], w=["GT"])
        barrier()

        stg = [RYf[:, i * D:(i + 1) * D] for i in range(8)]
        blocks = [(None, 0, NMETA)] + [(i, NMETA + 128 * i, 128) for i in range(SEQ // 128)]
        for bi, (xi, c0, n) in enumerate(blocks):
            s = bi % 8
            src = meta_d if xi is None else x_d[xi * 128:(xi + 1) * 128, :]
            P.op("sp", lambda e, s=s, src=src, n=n: e.dma_start(out=stg[s][0:n, :], in_=src),
                 w=[("stg", s)], dsem=f"ld{s}")
            for half in range(2):
                bk = 2 + (2 * bi + half) % 4
                for kk in range(4):
                    k = 4 * half + kk
                    P.op("pe", lambda e, s=s, n=n, k=k, kk=kk, bk=bk: e.transpose(
                        banks[bk][:, kk * 128:kk * 128 + n], stg[s][0:n, k * 128:(k + 1) * 128], ident[0:n, 0:n]),
                        r=[("stg", s), "cst"], w=[("pb", bk)])
                eng = "act" if half == 0 else "dve"
                src_ps = banks[bk][:, :].rearrange("p (a b) -> p a b", a=4)[:, :, 0:n]
                dst = h[:, 4 * half:4 * half + 4, c0:c0 + n]
                if eng == "act":
                    P.op("act", lambda e, dst=dst, src_ps=src_ps: e.activation(out=dst, in_=src_ps, func=AF.Copy),
                         r=[("pb", bk)], w=[("h", bi)])
                else:
                    P.op("dve", lambda e, dst=dst, src_ps=src_ps: e.tensor_copy(out=dst, in_=src_ps),
                         r=[("pb", bk)], w=[("h", bi)])
        barrier()

        def hkey(t):
            return ("hT", t)

        ALLT = list(range(5))
        HNK = [("hn", t) for t in ALLT]
        FSK = [("fst", t) for t in ALLT]
        RYK = [("act", t) for t in ALLT] + [("y", t) for t in ALLT] + ["u", "u0", "vt", ("qk", 0), ("qk", 1)]

        def post_norm_add(src_st, tiles_loc, vrow):
            for (t, off) in tiles_loc:
                c0, n = TILES[t]
                rs = rstd[t % 2]
                sv = src_st[:, :, off:off + n]
                norm_stats(sv, n, rs, [("fst", t)], D * NORM_EPS, bank=6)
                for m in range(KC):
                    P.op("dve", lambda e, m=m, off=off, n=n, rs=rs: e.scalar_tensor_tensor(
                        out=src_st[:, m, off:off + n], in0=src_st[:, m, off:off + n], scalar=gvec(vrow, m),
                        in1=rs[:, 0:n], op0=ALU.mult, op1=ALU.mult),
                        r=[("fst", t), ("rstd", id(rs)), "GT"], w=[("fst", t)])
                P.op("dve", lambda e, c0=c0, n=n, sv=sv: e.tensor_tensor(out=h[:, :, c0:c0 + n], in0=h[:, :, c0:c0 + n],
                                                                         in1=sv, op=ALU.add),
                     r=[("fst", t), hkey(t)], w=[hkey(t)])

        def pre_norm(tiles_loc, dst3, vrow, war):
            for (t, off) in tiles_loc:
                c0, n = TILES[t]
                rs = rstd[t % 2]
                norm_stats(h[:, :, c0:c0 + n], n, rs, [hkey(t)], D * NORM_EPS, bank=6)
                for k in range(KC):
                    P.op("dve", lambda e, k=k, c0=c0, n=n, off=off, rs=rs: e.scalar_tensor_tensor(
                        out=dst3[:, k, off:off + n], in0=h[:, k, c0:c0 + n], scalar=gvec(vrow, k),
                        in1=rs[:, 0:n], op0=ALU.mult, op1=ALU.mult),
                        r=[hkey(t), ("rstd", id(rs)), "GT"], w=[("hn", t)], war=(war if k == 0 else ()))

        def locs(S):
            out, off = [], 0
            for t in S:
                out.append((t, off))
                off += TILES[t][1]
            return out

        sched = {"pre_done": None}
        micro_pre = []
        micro_post = []

        def capture(q, fn):
            P.capture = []
            fn()
            q.extend(P.capture)
            P.capture = None

        def drain(q, n=None):
            while q and (n is None or n > 0):
                a, k = q.pop(0)
                P._op(*a, **k)
                if n is not None:
                    n -= 1

        def flush_deferred():
            drain(micro_pre)
            drain(micro_post)

        def ffn_pre(item, lazy):
            _, wgu, wdn, vpre, vpost, S = item
            if lazy:
                capture(micro_pre, lambda: pre_norm(locs(S), hn_st, vpre, HNK + HNK_MIX))
            else:
                pre_norm(locs(S), hn_st, vpre, HNK + HNK_MIX)
            sched["pre_done"] = item

        def ffn_pass(item, nxt):
            _, wgu, wdn, vpre, vpost, S = item
            wdn_r = wdn.rearrange("(j p) c -> p j c", p=128)
            tiles_loc = locs(S)
            if sched["pre_done"] is not item:
                flush_deferred()
                ffn_pre(item, False)
            drain(micro_pre)
            gi = 0
            first_act = {t: True for t in S}
            for j0 in range(0, JC, 4):
                nj = min(4, JC - j0)
                gsrc = wgu[:, j0 * 128:(j0 + nj) * 128].rearrange("(k p) c -> p k c", p=128)
                usrc = wgu[:, FF + j0 * 128:FF + (j0 + nj) * 128].rearrange("(k p) c -> p k c", p=128)
                s, (gv, uv) = wfill([(0, KC, nj * 128, gsrc), (4096, KC, nj * 128, usrc)])
                for jl in range(nj):
                    j = j0 + jl
                    for (t, off) in tiles_loc:
                        c0, n = TILES[t]
                        pb = 2 * (gi % 2)
                        gi += 1
                        gp, up = banks[pb], banks[pb + 1]
                        mm_group(gp[:, 0:n], [(gv[:, k, jl * 128:(jl + 1) * 128], hn_st[:, k, off:off + n])
                                              for k in range(KC)],
                                 r=[("ring", s), ("hn", t)], w=[("pb", pb)])
                        mm_group(up[:, 0:n], [(uv[:, k, jl * 128:(jl + 1) * 128], hn_st[:, k, off:off + n])
                                              for k in range(KC)],
                                 r=[("ring", s), ("hn", t)], w=[("pb", pb + 1)])
                        sg = sgt[(pb // 2) % 2]
                        P.op("act", lambda e, sg=sg, gp=gp, n=n: e.activation(out=sg[:, 0:n], in_=gp[:, 0:n],
                                                                              func=AF.Silu),
                             r=[("pb", pb)], w=[("sg", pb)])
                        P.op("dve", lambda e, sg=sg, up=up, n=n, j=j, off=off: e.tensor_tensor(
                            out=act[:, j, off:off + n], in0=sg[:, 0:n], in1=up[:, 0:n], op=ALU.mult),
                            r=[("sg", pb), ("pb", pb + 1)], w=[("act", t)], war=(RYK if first_act[t] else ()))
                        first_act[t] = False
                        drain(micro_post, 2)
            drain(micro_post)
            if nxt is not None and nxt[0] == "ffn":
                ffn_pre(nxt, True)
            elif nxt is not None and nxt[0] == "mix":
                pre_norm_mixer_A(8 + 2 * nxt[1], True)
            fi = 0
            first_f = {t: True for t in S}
            for m0 in range(0, KC, 2):
                s, (dv,) = wfill([(0, JC, 256, wdn_r[:, :, m0 * 128:(m0 + 2) * 128])])
                for ml in range(2):
                    m = m0 + ml
                    for (t, off) in tiles_loc:
                        c0, n = TILES[t]
                        pb = 4 + fi % 2
                        fi += 1
                        mm_group(banks[pb][:, 0:n], [(dv[:, j, ml * 128:(ml + 1) * 128], act[:, j, off:off + n])
                                                     for j in range(JC)],
                                 r=[("ring", s), ("act", t)], w=[("pb", pb)])
                        P.op("act", lambda e, pb=pb, m=m, off=off, n=n: e.activation(
                            out=f_st[:, m, off:off + n], in_=banks[pb][:, 0:n], func=AF.Copy),
                            r=[("pb", pb)], w=[("fst", t)], war=(FSK + HNK_MIX + ["rope"] if first_f[t] else ()))
                        first_f[t] = False
                        drain(micro_pre, 4 if len(S) == 3 else 5)
            capture(micro_post, lambda: post_norm_add(f_st, tiles_loc, vpost))

        HNK_MIX = [("hnm", t) for t in ALLT]

        def mixer_out(wsrc, vpost, ykey, nxt):
            s, (wv,) = wfill([(0, KC, D, wsrc.rearrange("(k p) c -> p k c", p=128))])
            fi = 0
            prev = None
            for S in SUPERS:
                tiles_loc = locs(S)
                if tiles_loc[0][0] == 0:
                    tiles_loc = tiles_loc[1:] + tiles_loc[:1]
                for idx, (t, off) in enumerate(tiles_loc):
                    c0, n = TILES[t]
                    if idx == 0 and prev is not None:
                        drain(micro_post)
                        post_norm_add(f_st, [prev], vpost)
                        prev = None
                        if nxt is not None and nxt[0] == "ffn":
                            ffn_pre(nxt, True)
                    if prev is not None:
                        drain(micro_pre)
                        capture(micro_post, lambda pv_=prev: post_norm_add(f_st, [pv_], vpost))
                        prev = None
                    for m in range(KC):
                        pb = 4 + fi % 2
                        fi += 1
                        mm_group(banks[pb][:, 0:n], [(wv[:, k, m * 128:(m + 1) * 128], y_all[:, k, c0:c0 + n])
                                                     for k in range(KC)],
                                 r=[("ring", s), (ykey, t)], w=[("pb", pb)])
                        P.op("act", lambda e, pb=pb, m=m, off=off, n=n: e.activation(
                            out=f_st[:, m, off:off + n], in_=banks[pb][:, 0:n], func=AF.Copy),
                            r=[("pb", pb)], w=[("fst", t)], war=(FSK + HNK_MIX + ["rope"] if m == 0 else ()))
                        if micro_post:
                            drain(micro_post, 4)
                        else:
                            drain(micro_pre, 4)
                    drain(micro_post)
                    prev = (t, off)
            capture(micro_post, lambda pv_=prev: post_norm_add(f_st, [pv_], vpost))

        all_tiles = [(t, TILES[t][0]) for t in range(5)]

        def pre_norm_mixer_A(vpre, lazy):
            fn = lambda: pre_norm(locs(SUPERS[0]), hn_st, vpre, HNK + HNK_MIX)
            if lazy:
                capture(micro_pre, fn)
            else:
                fn()
            sched["pre_done"] = ("mixA", vpre)

        def pre_norm_mixer_B(vpre):
            for t in SUPERS[1]:
                c0, n = TILES[t]
                rs = rstd[t % 2]
                norm_stats(h[:, :, c0:c0 + n], n, rs, [hkey(t)], D * NORM_EPS, bank=6)
                for k in range(KC):
                    P.op("dve", lambda e, k=k, c0=c0, n=n, rs=rs: e.scalar_tensor_tensor(
                        out=hnv(c0, n)[:, k, :], in0=h[:, k, c0:c0 + n], scalar=gvec(vpre, k),
                        in1=rs[:, 0:n], op0=ALU.mult, op1=ALU.mult),
                        r=[hkey(t), ("rstd", id(rs)), "GT"], w=[("hnm", t)], war=(FSK + HNK_MIX + ["rope"] if k == 0 else ()))

        def mixer_pre(vpre):
            if sched["pre_done"] != ("mixA", vpre):
                flush_deferred()
                pre_norm_mixer_A(vpre, False)
            drain(micro_pre)

        def conv_mixer(j, vpre, vpost, nxt):
            win = cin_d[j]
            mixer_pre(vpre)
            pend_B = [True]
            P.op("dve", lambda e: e.memset(u_buf[:, 0:2], 0.0), w=["u0"], war=RYK)
            cw = lambda tap, m: GT[:, m, 24 + 3 * j + tap:24 + 3 * j + tap + 1]
            gi = 0
            first_y = {t: True for t in ALLT}
            for m0 in range(0, KC, 2):
                srcs = [win[:, sec * D + m0 * 128: sec * D + (m0 + 2) * 128].rearrange("(k p) c -> p k c", p=128)
                        for sec in range(3)]
                s, wv = wfill([(sec * 2048, KC, 256, srcs[sec]) for sec in range(3)])
                for ml in range(2):
                    m = m0 + ml
                    for (t, c0) in all_tiles:
                        n = TILES[t][1]
                        if t == 3 and pend_B[0]:
                            drain(micro_post)
                            pre_norm_mixer_B(vpre)
                            pend_B[0] = False
                        if t == 0:
                            pbs = [7, 7, 7]
                            pvs = [banks[7][:, 16 * sec:16 * sec + n] for sec in range(3)]
                        else:
                            pbs = [3 * (gi % 2), 3 * (gi % 2) + 1, 3 * (gi % 2) + 2]
                            pvs = [banks[q][:, 0:n] for q in pbs]
                            gi += 1
                        for sec in range(3):
                            mm_group(pvs[sec],
                                     [(wv[sec][:, k, ml * 128:(ml + 1) * 128], hnv(c0, n)[:, k, :]) for k in range(KC)],
                                     r=[("ring", s), hnkey(t)], w=[("pb", pbs[sec])])
                        if pend_B[0]:
                            drain(micro_post, 10)
                        bp, cp, xp = pvs
                        xs = sgt[gi % 2]
                        P.op("act", lambda e, xs=xs, xp=xp, n=n: e.activation(out=xs[:, 0:n], in_=xp[:, 0:n], func=AF.Copy),
                             r=[("pb", pbs[2])], w=[("sg", 2 * (gi % 2))])
                        P.op("dve", lambda e, xs=xs, cp=cp, n=n, c0=c0: e.tensor_tensor(
                            out=u_buf[:, 2 + c0:2 + c0 + n], in0=xs[:, 0:n], in1=cp[:, 0:n], op=ALU.mult),
                            r=[("sg", 2 * (gi % 2)), ("pb", pbs[1]), "u0"], w=["u"])
                        P.op("dve", lambda e, m=m, n=n, c0=c0: e.tensor_scalar(
                            out=vtmp[:, 0:n], in0=u_buf[:, c0:c0 + n], scalar1=cw(0, m), scalar2=None, op0=ALU.mult),
                            r=["u", "GT"], w=["vtmp"])
                        P.op("dve", lambda e, m=m, n=n, c0=c0: e.scalar_tensor_tensor(
                            out=vtmp[:, 0:n], in0=u_buf[:, 1 + c0:1 + c0 + n], scalar=cw(1, m), in1=vtmp[:, 0:n],
                            op0=ALU.mult, op1=ALU.add), r=["u", "vtmp"], w=["vtmp"])
                        P.op("dve", lambda e, m=m, n=n, c0=c0: e.scalar_tensor_tensor(
                            out=vtmp[:, 0:n], in0=u_buf[:, 2 + c0:2 + c0 + n], scalar=cw(2, m), in1=vtmp[:, 0:n],
                            op0=ALU.mult, op1=ALU.add), r=["u", "vtmp"], w=["vtmp"])
                        P.op("dve", lambda e, m=m, n=n, c0=c0, bp=bp: e.tensor_tensor(
                            out=y_all[:, m, c0:c0 + n], in0=vtmp[:, 0:n], in1=bp[:, 0:n], op=ALU.mult),
                            r=["vtmp", ("pb", pbs[0])], w=[("y", t)], war=(RYK if first_y[t] else ()))
                        first_y[t] = False
            mixer_out(cout_d[j], vpost, "y", nxt)

        def attn_mixer(j, li_idx, vpre, vpost, nxt):
            wqkv = qkv_d[j]
            neglam = lamt[:, 4 + j:5 + j]
            sgain = GT[:, 0, 30 + j:31 + j]
            mixer_pre(vpre)
            drain(micro_post)
            pre_norm_mixer_B(vpre)
            barrier()
            P.op("sp", lambda e: e.dma_start(out=rope_sb, in_=rope_d),
                 w=["rope"], war=FSK, dsem="rp")
            gstep = [0]
            carry = [None]
            first_y = {t: True for t in ALLT}
            for hp in range(NH // 2):
                srcs = [wqkv[:, sec * D + hp * 256: sec * D + (hp + 1) * 256].rearrange("(k p) c -> p k c", p=128)
                        for sec in range(3)]
                s, wv = wfill([(sec * 2048, KC, 256, srcs[sec]) for sec in range(3)])
                for hl in range(2):
                    hd = 2 * hp + hl

                    porder = [1, 2, 3, 4, 0]

                    def proj(ti):
                        t, c0 = all_tiles[porder[ti]]
                        n = TILES[t][1]
                        par = (ti + 1) % 2
                        for sec in range(2):
                            pb = 2 * par + sec
                            mm_group(banks[pb][:, 0:n],
                                     [(wv[sec][:, k, hl * 128:(hl + 1) * 128], hnv(c0, n)[:, k, :]) for k in range(KC)],
                                     r=[("ring", s), hnkey(t)], w=[("pb", pb)])
                            tb = (tq, tk)[sec][par]
                            P.op("act", lambda e, pb=pb, n=n, tb=tb: e.activation(out=tb[:, 0:n], in_=banks[pb][:, 0:n],
                                                                                  func=AF.Copy),
                                 r=[("pb", pb)], w=[("tqk", sec, par)])

                    def rot(ti):
                        t, c0 = all_tiles[porder[ti]]
                        n = TILES[t][1]
                        par = (ti + 1) % 2
                        for sec, dst in ((0, qr), (1, kr)):
                            pb = 4 + 2 * par + sec
                            tb = (tq, tk)[sec][par]
                            P.op("pe", lambda e, pb=pb, n=n, tb=tb: e.matmul(banks[pb][:, 0:n], rmat, tb[:, 0:n],
                                                                             start=True, stop=True),
                                 r=[("tqk", sec, par), "cst"], w=[("pb", pb)])
                            P.op("dve", lambda e, n=n, tb=tb, c0=c0: e.tensor_tensor(
                                out=tb[:, 0:n], in0=tb[:, 0:n], in1=rope_sb[:, 0, c0:c0 + n], op=ALU.mult),
                                r=[("tqk", sec, par), "rope"], w=[("tqk", sec, par)])
                            P.op("dve", lambda e, pb=pb, n=n, c0=c0: e.tensor_tensor(
                                out=banks[pb][:, 0:n], in0=banks[pb][:, 0:n], in1=rope_sb[:, 1, c0:c0 + n], op=ALU.mult),
                                r=[("pb", pb), "rope"], w=[("pb", pb)])
                            P.op("dve", lambda e, pb=pb, n=n, c0=c0, dst=dst, tb=tb: e.tensor_tensor(
                                out=dst[:, c0:c0 + n], in0=tb[:, 0:n], in1=banks[pb][:, 0:n], op=ALU.add),
                                r=[("tqk", sec, par), ("pb", pb)], w=[("qk", sec)], war=(RYK if ti == 0 and hd == 0 else ()))

                    proj(0)
                    if carry[0] is not None:
                        carry[0]()
                        carry[0] = None
                    for ti in range(5):
                        if ti + 1 < 5:
                            proj(ti + 1)
                        rot(ti)
                    for g0 in range(0, 17, 4):
                        kts = list(range(g0, min(17, g0 + 4)))
                        pb = 0 if (g0 // 4) % 2 == 0 else 2
                        for ii, kt in enumerate(kts):
                            kc0, kn = KT[kt]
                            mm_group(banks[pb][0:kn, ii * 128:(ii + 1) * 128],
                                     [(hnv(kc0, kn)[:, k, :], wv[2][:, k, hl * 128:(hl + 1) * 128]) for k in range(KC)],
                                     r=[("ring", s)] + [hnkey(t_) for t_ in ALLT], w=[("pb", pb)])
                        for ii, kt in enumerate(kts):
                            kn = KT[kt][1]
                            P.op("act", lambda e, pb=pb, ii=ii, kt=kt, kn=kn: e.activation(
                                out=vt[0:kn, kt, :], in_=banks[pb][0:kn, ii * 128:(ii + 1) * 128], func=AF.Copy),
                                r=[("pb", pb)], w=["vt"], war=(RYK if hd == 0 and kt == 0 else ()))
                    steps = []
                    for qt in range(5):
                        qc0, qn = TILES[qt]
                        if qt == 0:
                            st = [(0, 0, qn, True)]
                        else:
                            st = [(0, 0, qn, False)]
                            st += [(1 + kt, 0, qn, False) for kt in range(4 * (qt - 1))]
                            st += [(1 + 4 * (qt - 1) + r_, 128 * r_, qn - 128 * r_, True) for r_ in range(4)]
                        for si, (kt, lo, wd, diag) in enumerate(st):
                            steps.append((qt, si, len(st), kt, lo, wd, diag))

                    def scores(gi_, step):
                        qt, si, nst, kt, lo, wd, diag = step
                        qc0, qn = TILES[qt]
                        kc0, kn = KT[kt]
                        sb_ = gi_ % 2
                        for c in range(2):
                            pb = 2 * sb_ + c
                            P.op("pe", lambda e, pb=pb, c=c, kc0=kc0, kn=kn, lo=lo, wd=wd, qc0=qc0: e.matmul(
                                banks[pb][0:kn, 0:wd], kr[c * 64:(c + 1) * 64, kc0:kc0 + kn],
                                qr[c * 64:(c + 1) * 64, qc0 + lo:qc0 + lo + wd], start=True, stop=True),
                                r=[("qk", 0), ("qk", 1)], w=[("pb", pb)])
                            pt = pT[pb]
                            P.op("act", lambda e, pb=pb, pt=pt, kn=kn, wd=wd: e.activation(
                                out=pt[0:kn, 0:wd], in_=banks[pb][0:kn, 0:wd], func=AF.Exp, scale=HD ** -0.5),
                                r=[("pb", pb)], w=[("pT", pb)])
                            if diag:
                                dn = min(kn, wd)
                                P.op("dve", lambda e, pt=pt, kn=kn, dn=dn: e.tensor_tensor(
                                    out=pt[0:kn, 0:dn], in0=pt[0:kn, 0:dn], in1=trib[0:kn, 0:dn], op=ALU.mult),
                                    r=[("pT", pb), "trib"], w=[("pT", pb)])

                    def pv(gi_, step):
                        qt, si, nst, kt, lo, wd, diag = step
                        kc0, kn = KT[kt]
                        sb_ = gi_ % 2
                        for c in range(2):
                            pb = 2 * sb_ + c
                            pt = pT[pb]
                            P.op("pe", lambda e, c=c, pt=pt, kt=kt, kn=kn, lo=lo, wd=wd, si=si, nst=nst: e.matmul(
                                banks[4 + c][:, lo:lo + wd], vt[0:kn, kt, :], pt[0:kn, 0:wd],
                                start=(si == 0), stop=(si == nst - 1)),
                                r=[("pT", pb), "vt"], w=[("pb", 4 + c)])
                            P.op("pe", lambda e, c=c, pt=pt, kn=kn, lo=lo, wd=wd, si=si, nst=nst: e.matmul(
                                banks[6 + c][:, lo:lo + wd], onesb[0:kn, :], pt[0:kn, 0:wd],
                                start=(si == 0), stop=(si == nst - 1)),
                                r=[("pT", pb), "onesb"], w=[("pb", 6 + c)])

                    def finalize(gi_, qt):
                        qc0, qn = TILES[qt]
                        k0, k1 = ("tqk", 0, 0), ("tqk", 0, 1)
                        o0, o1 = tk[0], tk[1]
                        ko0, ko1 = ("tqk", 1, 0), ("tqk", 1, 1)
                        P.op("act", lambda e, qn=qn: e.activation(out=o0[:, 0:qn], in_=banks[4][:, 0:qn], func=AF.Copy),
                             r=[("pb", 4)], w=[ko0])
                        P.op("dve", lambda e, qn=qn: e.tensor_copy(out=o1[:, 0:qn], in_=banks[5][:, 0:qn]),
                             r=[("pb", 5)], w=[ko1])
                        P.op("act", lambda e, qn=qn: e.activation(out=t1[:, 0:qn], in_=banks[7][:, 0:qn], func=AF.Copy),
                             r=[("pb", 7)], w=[k1])
                        P.op("dve", lambda e, qn=qn: e.tensor_copy(out=t0[:, 0:qn], in_=banks[6][:, 0:qn]),
                             r=[("pb", 6)], w=[k0])
                        P.op("dve", lambda e, qn=qn: e.tensor_tensor(out=o0[:, 0:qn], in0=o0[:, 0:qn], in1=t1[:, 0:qn],
                                                                     op=ALU.mult), r=[k1, ko0], w=[ko0])
                        P.op("dve", lambda e, qn=qn: e.tensor_tensor(out=o1[:, 0:qn], in0=o1[:, 0:qn], in1=t0[:, 0:qn],
                                                                     op=ALU.mult), r=[k0, ko1], w=[ko1])
                        P.op("dve", lambda e, qn=qn: e.scalar_tensor_tensor(out=o0[:, 0:qn], in0=o1[:, 0:qn], scalar=neglam,
                                                                            in1=o0[:, 0:qn], op0=ALU.mult, op1=ALU.add),
                             r=[ko0, ko1, "lamt"], w=[ko0])
                        P.op("act", lambda e, qn=qn: e.activation(out=osq[:, 0:qn], in_=o0[:, 0:qn], func=AF.Square),
                             r=[ko0], w=["osq"])
                        P.op("dve", lambda e, qn=qn: e.tensor_tensor(out=t0[:, 0:qn], in0=t0[:, 0:qn], in1=t1[:, 0:qn],
                                                                     op=ALU.mult), r=[k0, k1], w=[k0])
                        P.op("dve", lambda e, qn=qn: e.scalar_tensor_tensor(out=t0[:, 0:qn], in0=t0[:, 0:qn],
                                                                            scalar=128.0 * SUBLN_EPS, in1=t0[:, 0:qn],
                                                                            op0=ALU.mult, op1=ALU.mult),
                             r=[k0], w=[k0])

                    def finalize2(gi_, qt, hd=hd, sbk=None):
                        qc0, qn = TILES[qt]
                        k0 = ("tqk", 0, 0)
                        o0 = tk[0]
                        ko0 = ("tqk", 1, 0)
                        if sbk is None:
                            sbk = 2 * (gi_ % 2)
                        P.op("pe", lambda e, qn=qn, sbk=sbk: e.matmul(banks[sbk][:, 0:qn], onesb[:, :], osq[:, 0:qn],
                                                                      start=True, stop=True),
                             r=["osq", "onesb"], w=[("pb", sbk)])
                        P.op("dve", lambda e, qn=qn, sbk=sbk: e.tensor_tensor(out=t0[:, 0:qn], in0=t0[:, 0:qn],
                                                                              in1=banks[sbk][:, 0:qn], op=ALU.add),
                             r=[k0, ("pb", sbk)], w=[k0])
                        P.op("act", lambda e, qn=qn: e.activation(out=rstd_a[:, 0:qn], in_=t0[:, 0:qn],
                                                                  func=AF.Ln, scale=2.0 ** -20),
                             r=[k0], w=["rstd_a"])
                        P.op("act", lambda e, qn=qn: e.activation(out=rstd_a[:, 0:qn], in_=rstd_a[:, 0:qn],
                                                                  func=AF.Exp, scale=-0.5, bias=-10.0 * math.log(2.0)),
                             r=["rstd_a"], w=["rstd_a"])
                        P.op("dve", lambda e, qn=qn, qc0=qc0, hd=hd: e.scalar_tensor_tensor(
                            out=y_all[:, hd, qc0:qc0 + qn], in0=o0[:, 0:qn], scalar=sgain, in1=rstd_a[:, 0:qn],
                            op0=ALU.mult, op1=ALU.mult), r=[ko0, "rstd_a", "GT"], w=[("y", qt)],
                            war=(RYK if first_y[qt] else ()))
                        first_y[qt] = False

                    g0_ = gstep[0]
                    scores(g0_, steps[0])
                    pend = None
                    for i, step in enumerate(steps):
                        if i + 1 < len(steps):
                            scores(g0_ + i + 1, steps[i + 1])
                        pv(g0_ + i, step)
                        if pend is not None and i >= pend[0]:
                            finalize2(pend[1], pend[2])
                            pend = None
                        if step[1] == step[2] - 1:
                            finalize(g0_ + i, step[0])
                            pend = (i + 3, g0_ + i + 3, step[0])
                    if pend is not None:
                        carry[0] = (lambda f2=finalize2, a=pend[1], b=pend[2]: f2(a, b, sbk=6))
                    gstep[0] += len(steps)
            if carry[0] is not None:
                carry[0]()
                carry[0] = None
            barrier()
            if dbg:
                P.op("pool", lambda e: e.dma_start(out=dbg_d, in_=RYb[:, 0:KC * NT]), r=[("y", t) for t in range(5)], dsem="dbg")
                P.op("pool", lambda e: e.nop(), w=[("y", t) for t in range(5)])
            mixer_out(wo_d[j], vpost, "y", nxt)

        steps = []
        for i in range(depth):
            steps.append(("ffn1", i))
            steps.append(("mix", i))
            steps.append(("ffn2", i))
        if nsteps is not None:
            steps = steps[:nsteps]
        items = []
        for kind, i in steps:
            if kind == "ffn1":
                items += [("ffn", gu1_d[i], dn1_d[i], 2 * i, 2 * i + 1, S) for S in SUPERS]
            elif kind == "ffn2":
                items += [("ffn", gu2_d[i], dn2_d[i], 16 + 2 * i, 16 + 2 * i + 1, S) for S in SUPERS]
            else:
                items.append(("mix", i))
        for ii, item in enumerate(items):
            nxt = items[ii + 1] if ii + 1 < len(items) else None
            if item[0] == "ffn":
                ffn_pass(item, nxt)
            else:
                i = item[1]
                if i % 2 == 0:
                    conv_mixer(i // 2, 8 + 2 * i, 8 + 2 * i + 1, nxt)
                else:
                    attn_mixer(i // 2, i, 8 + 2 * i, 8 + 2 * i + 1, nxt)
        flush_deferred()

        barrier()
        ostg = [RYf[:, i * D:(i + 1) * D] for i in range(8)]
        for bi in range(SEQ // 128):
            c0 = NMETA + 128 * bi
            s = bi % 8
            tq = 1 + bi // 4
            for half in range(2):
                bk = (2 * bi + half) % 4
                for kk in range(4):
                    k = 4 * half + kk
                    P.op("pe", lambda e, k=k, kk=kk, bk=bk, c0=c0: e.transpose(
                        banks[bk][:, kk * 128:(kk + 1) * 128], h[:, k, c0:c0 + 128], ident),
                        r=[hkey(tq), "cst"], w=[("pb", bk)])
                dst = ostg[s][:, 512 * half:512 * (half + 1)]
                if half == 0:
                    P.op("act", lambda e, dst=dst, bk=bk: e.activation(out=dst, in_=banks[bk][:, :], func=AF.Copy),
                         r=[("pb", bk)], w=[("ostg", s)])
                else:
                    P.op("dve", lambda e, dst=dst, bk=bk: e.tensor_copy(out=dst, in_=banks[bk][:, :]),
                         r=[("pb", bk)], w=[("ostg", s)])
            P.op("sp", lambda e, s=s, bi=bi: e.dma_start(out=out_d[bi * 128:(bi + 1) * 128, :], in_=ostg[s]),
                 r=[("ostg", s)], dsem=f"st{s}")
        for s in range(8):
            P.op("sp", lambda e: e.nop(), w=[("ostg", s)])

        P.finalize()
        sems = {e: es.enter_context(nc.semaphore(f"s_{e}")) for e in ENGS}
        dsems = {n: es.enter_context(nc.semaphore(f"d_{n}")) for n in P.dma_cnt}
        with nc.Block() as block:
            @block.sync
            def _(e):
                P.replay("sp", e, sems, dsems)

            @block.scalar
            def _(e):
                P.replay("act", e, sems, dsems)

            @block.vector
            def _(e):
                P.replay("dve", e, sems, dsems)

            @block.gpsimd
            def _(e):
                P.replay("pool", e, sems, dsems)

            @block.tensor
            def _(e):
                P.replay("pe", e, sems, dsems)
    return nc


def _consts():
    ident = np.eye(128, dtype=np.float32)
    rmat = np.zeros((128, 128), np.float32)
    for c in range(2):
        for d in range(8):
            rmat[c * 64 + d + 8, c * 64 + d] = -1.0
            rmat[c * 64 + d, c * 64 + d + 8] = 1.0
    tri = (np.arange(128)[None, :] >= np.arange(128)[:, None]).astype(np.float32)
    pos = np.arange(NT, dtype=np.float32)
    inv_freq = (np.float32(THETA) ** (-np.arange(0, ROT, 2, dtype=np.float32) / np.float32(ROT))).astype(np.float32)
    ang = (pos[:, None] * inv_freq[None, :]).astype(np.float32)
    cos = np.cos(ang).astype(np.float32)
    sin = np.sin(ang).astype(np.float32)
    rope = np.zeros((128, 2, NT), np.float32)
    rope[:, 0, :] = 1.0
    for c in range(2):
        for d in range(16):
            rope[c * 64 + d, 0, :] = cos[:, d % 8]
            rope[c * 64 + d, 1, :] = sin[:, d % 8]
    return np.concatenate([ident, rmat], axis=1), tri, rope


_CACHE = {}


def kernel(x, meta_tokens, ln_ffn1, ffn1_w_gu, ffn1_w_down, ln_mix, conv_w_in, conv_w, conv_w_out,
           attn_w_qkv, attn_lambda, attn_subln_g, attn_w_o, ln_ffn2, ffn2_w_gu, ffn2_w_down,
           _depth=DEPTH, _nsteps=None, _dbg=False):
    f = lambda a: np.ascontiguousarray(np.asarray(a, dtype=np.float32))
    x = f(x)
    nb = x.shape[0]
    vecs = np.zeros((32, D), np.float32)
    vecs[0:8] = f(ln_ffn1).reshape(8, D)
    vecs[8:16] = f(ln_mix).reshape(8, D)
    vecs[16:24] = f(ln_ffn2).reshape(8, D)
    vecs[24:30] = f(conv_w).reshape(6, D)
    vecs[30:32, 0:128] = f(attn_subln_g)
    cst, tri, rope = _consts()
    shared = {
        "meta": f(meta_tokens), "vecs": vecs, "lam": f(attn_lambda).reshape(1, 512),
        "ffn1_w_gu": f(ffn1_w_gu), "ffn1_w_down": f(ffn1_w_down),
        "ffn2_w_gu": f(ffn2_w_gu), "ffn2_w_down": f(ffn2_w_down),
        "conv_w_in": f(conv_w_in), "conv_w_out": f(conv_w_out),
        "attn_w_qkv": f(attn_w_qkv), "attn_w_o": f(attn_w_o),
        "rope": rope, "consts": cst, "tri": tri,
    }
    key = (_depth, _nsteps, _dbg)
    if key not in _CACHE:
        _CACHE[key] = build_program(_depth, _nsteps, _dbg)
    nc = _CACHE[key]
    in_maps = [dict(shared, x=x[b]) for b in range(nb)]
    res = run_bass_kernel_spmd(nc, in_maps, core_ids=list(range(nb)))
    if _dbg:
        return np.stack([r["out"] for r in res.results], axis=0).astype(np.float32), res.results[0]["dbg"]
    return np.stack([r["out"] for r in res.results], axis=0).astype(np.float32)
```
